# Optimizing a Trainium2 kernel written in Bass

```python
import math
import jax, jax.numpy as jnp
from jax import lax
import numpy as np

D_MODEL = 1024
BATCH = 16
SEQ = 2048
DEPTH = 4
DEC_BATCH = 8
DEC_SEQ = 8192
PAST_LEN = 128

PLE_DIM = 256
N_HEADS = 8
HEAD_DIM = 64
V_DIM = 2 * HEAD_DIM
QK_WIDTH = N_HEADS * 2 * HEAD_DIM
ATTN_WIDTH = N_HEADS * V_DIM
CONV_WIDTH = D_MODEL
CONV_KERNEL = 31
D_FF = 2816
ROPE_THETA = 10000.0
Q_BLOCK = 128
LN_EPS = 1e-5
DEEPNORM_ALPHA = (2 * DEPTH) ** 0.25
DEEPNORM_BETA = (8 * DEPTH) ** -0.25
IN_SPLITS = (QK_WIDTH, 2 * QK_WIDTH, 2 * QK_WIDTH + ATTN_WIDTH,
             2 * QK_WIDTH + ATTN_WIDTH + 2 * CONV_WIDTH)
IN_WIDTH = 2 * QK_WIDTH + ATTN_WIDTH + 2 * CONV_WIDTH + 2 * D_MODEL

kernel_name = "hybrid_diffattn_conformer_encoder"


def layer_norm(x, g, b):
    xf = x.astype(jnp.float32)
    mu = jnp.mean(xf, axis=-1, keepdims=True)
    var = jnp.mean(jnp.square(xf - mu), axis=-1, keepdims=True)
    y = (xf - mu) * lax.rsqrt(var + LN_EPS)
    return (y * g.astype(jnp.float32) + b.astype(jnp.float32)).astype(x.dtype)


def rms_norm(x, g):
    xf = x.astype(jnp.float32)
    y = xf * lax.rsqrt(jnp.mean(jnp.square(xf), axis=-1, keepdims=True) + LN_EPS)
    return (y * g.astype(jnp.float32)).astype(x.dtype)


def swiglu(x, w_gate, w_up, w_down):
    return (jax.nn.silu(x @ w_gate) * (x @ w_up)) @ w_down


def rope(x, seq_len):
    half = HEAD_DIM // 2
    inv_freq = ROPE_THETA ** (-jnp.arange(half, dtype=jnp.float32) / half)
    ang = jnp.arange(seq_len, dtype=jnp.float32)[:, None] * inv_freq[None, :]
    cos = jnp.cos(ang)[None, :, None, None, :]
    sin = jnp.sin(ang)[None, :, None, None, :]
    xf = x.astype(jnp.float32)
    x1, x2 = xf[..., :half], xf[..., half:]
    return jnp.concatenate([x1 * cos - x2 * sin, x2 * cos + x1 * sin], axis=-1).astype(x.dtype)


def diff_attention(q, k, v, lam):
    B, S = q.shape[0], q.shape[1]
    nb = S // Q_BLOCK
    scale = HEAD_DIM ** -0.5
    qb = q.reshape(B, nb, Q_BLOCK, N_HEADS, 2, HEAD_DIM).transpose(1, 0, 2, 3, 4, 5)

    def block(q_blk):
        s = jnp.einsum('bqhcd,bkhcd->bhcqk', q_blk, k).astype(jnp.float32) * scale
        probs = jax.nn.softmax(s, axis=-1)
        a = probs[:, :, 0] - lam * probs[:, :, 1]
        return jnp.einsum('bhqk,bkhe->bqhe', a.astype(v.dtype), v)

    out = lax.map(block, qb)
    return out.transpose(1, 0, 2, 3, 4).reshape(B, S, N_HEADS, V_DIM)


def depthwise_conv(x, w, b):
    pad = CONV_KERNEL // 2
    y = lax.conv_general_dilated(x, w[:, None, :].astype(x.dtype), window_strides=(1,),
                                 padding=[(pad, pad)],
                                 dimension_numbers=('NWC', 'WIO', 'NWC'),
                                 feature_group_count=CONV_WIDTH)
    return y + b


def token_mixing(x, layer_idx, w_in, lam_q1, lam_k1, lam_q2, lam_k2, subln_g,
                 conv_dw, conv_dw_b, conv_ln_g, conv_ln_b, conv_pw2, w_o):
    B, S, _ = x.shape
    z = x @ w_in
    q, k, v, u, gates = jnp.split(z, IN_SPLITS, axis=-1)
    q = rope(q.reshape(B, S, N_HEADS, 2, HEAD_DIM), S)
    k = rope(k.reshape(B, S, N_HEADS, 2, HEAD_DIM), S)
    v = v.reshape(B, S, N_HEADS, V_DIM)
    lam_init = 0.8 - 0.6 * math.exp(-0.3 * layer_idx)
    lam = (jnp.exp(jnp.sum(lam_q1.astype(jnp.float32) * lam_k1.astype(jnp.float32)))
           - jnp.exp(jnp.sum(lam_q2.astype(jnp.float32) * lam_k2.astype(jnp.float32)))
           + lam_init)
    o = diff_attention(q, k, v, lam)
    o = rms_norm(o, subln_g) * (1.0 - lam_init)
    attn_out = o.reshape(B, S, ATTN_WIDTH)
    c = u[..., :CONV_WIDTH] * jax.nn.sigmoid(u[..., CONV_WIDTH:])
    c = depthwise_conv(c, conv_dw, conv_dw_b)
    c = jax.nn.silu(layer_norm(c, conv_ln_g, conv_ln_b))
    conv_out = c @ conv_pw2
    g = jax.nn.sigmoid(gates)
    merged = g[..., :D_MODEL] * attn_out + g[..., D_MODEL:] * conv_out
    return merged @ w_o


def trunk(x, p, ffn1_w_gate, ffn1_w_up, ffn1_w_down, ln1_g, ln1_b,
          w_in, lam_q1, lam_k1, lam_q2, lam_k2, subln_g,
          conv_dw, conv_dw_b, conv_ln_g, conv_ln_b, conv_pw2, w_o, ln2_g, ln2_b,
          ffn2_w_gate, ffn2_w_up, ffn2_w_down, ple_w_gate, ple_w_proj, ln3_g, ln3_b):
    for i in range(DEPTH):
        x = layer_norm(DEEPNORM_ALPHA * x + 0.5 * swiglu(x, ffn1_w_gate[i], ffn1_w_up[i], ffn1_w_down[i]),
                       ln1_g[i], ln1_b[i])
        mix = token_mixing(x, i, w_in[i], lam_q1[i], lam_k1[i], lam_q2[i], lam_k2[i], subln_g[i],
                           conv_dw[i], conv_dw_b[i], conv_ln_g[i], conv_ln_b[i], conv_pw2[i], w_o[i])
        x = layer_norm(DEEPNORM_ALPHA * x + mix, ln2_g[i], ln2_b[i])
        ple = jax.nn.sigmoid(x @ ple_w_gate[i]) * (p[i] @ ple_w_proj[i])
        x = layer_norm(DEEPNORM_ALPHA * x + 0.5 * swiglu(x, ffn2_w_gate[i], ffn2_w_up[i], ffn2_w_down[i]) + ple,
                       ln3_g[i], ln3_b[i])
    return x


def setup_inputs(seed: int = 0) -> dict:
    key = jax.random.key(seed)
    ks = jax.random.split(key, 32)
    f32 = jnp.float32

    def nrm(k, shape, scale):
        return jax.random.normal(k, shape, dtype=f32) * scale

    def gain(k, shape):
        return 1.0 + 0.02 * jax.random.normal(k, shape, dtype=f32)

    L, D, F, C = DEPTH, D_MODEL, D_FF, CONV_WIDTH
    return {
        "x_prompt": nrm(ks[0], (BATCH, SEQ, D), 1.0),
        "x_sample": nrm(ks[1], (DEC_BATCH, DEC_SEQ, D), 1.0),
        "p_prompt": nrm(ks[2], (DEPTH, BATCH, SEQ, PLE_DIM), 1.0),
        "p_sample": nrm(ks[3], (DEPTH, DEC_BATCH, DEC_SEQ, PLE_DIM), 1.0),
        "ffn1_w_gate": nrm(ks[4], (L, D, F), D ** -0.5),
        "ffn1_w_up": nrm(ks[5], (L, D, F), D ** -0.5),
        "ffn1_w_down": nrm(ks[6], (L, F, D), F ** -0.5 * DEEPNORM_BETA),
        "ln1_g": gain(ks[7], (L, D)),
        "ln1_b": nrm(ks[8], (L, D), 0.02),
        "w_in": nrm(ks[9], (L, D, IN_WIDTH), D ** -0.5),
        "lam_q1": nrm(ks[10], (L, HEAD_DIM), 0.1),
        "lam_k1": nrm(ks[11], (L, HEAD_DIM), 0.1),
        "lam_q2": nrm(ks[12], (L, HEAD_DIM), 0.1),
        "lam_k2": nrm(ks[13], (L, HEAD_DIM), 0.1),
        "subln_g": gain(ks[14], (L, V_DIM)),
        "conv_dw": nrm(ks[15], (L, CONV_KERNEL, C), CONV_KERNEL ** -0.5),
        "conv_dw_b": nrm(ks[16], (L, C), 0.02),
        "conv_ln_g": gain(ks[17], (L, C)),
        "conv_ln_b": nrm(ks[18], (L, C), 0.02),
        "conv_pw2": nrm(ks[19], (L, C, D), C ** -0.5),
        "w_o": nrm(ks[20], (L, D, D), D ** -0.5 * DEEPNORM_BETA),
        "ln2_g": gain(ks[21], (L, D)),
        "ln2_b": nrm(ks[22], (L, D), 0.02),
        "ffn2_w_gate": nrm(ks[23], (L, D, F), D ** -0.5),
        "ffn2_w_up": nrm(ks[24], (L, D, F), D ** -0.5),
        "ffn2_w_down": nrm(ks[25], (L, F, D), F ** -0.5 * DEEPNORM_BETA),
        "ple_w_gate": nrm(ks[26], (L, D, D), D ** -0.5),
        "ple_w_proj": nrm(ks[27], (L, PLE_DIM, D), PLE_DIM ** -0.5 * DEEPNORM_BETA),
        "ln3_g": gain(ks[28], (L, D)),
        "ln3_b": nrm(ks[29], (L, D), 0.02),
    }


def reference(x_prompt, x_sample, p_prompt, p_sample,
              ffn1_w_gate, ffn1_w_up, ffn1_w_down, ln1_g, ln1_b,
              w_in, lam_q1, lam_k1, lam_q2, lam_k2, subln_g,
              conv_dw, conv_dw_b, conv_ln_g, conv_ln_b, conv_pw2, w_o, ln2_g, ln2_b,
              ffn2_w_gate, ffn2_w_up, ffn2_w_down, ple_w_gate, ple_w_proj, ln3_g, ln3_b):
    y_prompt = trunk(x_prompt, p_prompt, ffn1_w_gate, ffn1_w_up, ffn1_w_down, ln1_g, ln1_b,
                     w_in, lam_q1, lam_k1, lam_q2, lam_k2, subln_g,
                     conv_dw, conv_dw_b, conv_ln_g, conv_ln_b, conv_pw2, w_o, ln2_g, ln2_b,
                     ffn2_w_gate, ffn2_w_up, ffn2_w_down, ple_w_gate, ple_w_proj, ln3_g, ln3_b)
    y_sample = trunk(x_sample, p_sample, ffn1_w_gate, ffn1_w_up, ffn1_w_down, ln1_g, ln1_b,
                     w_in, lam_q1, lam_k1, lam_q2, lam_k2, subln_g,
                     conv_dw, conv_dw_b, conv_ln_g, conv_ln_b, conv_pw2, w_o, ln2_g, ln2_b,
                     ffn2_w_gate, ffn2_w_up, ffn2_w_down, ple_w_gate, ple_w_proj, ln3_g, ln3_b)
    return (y_prompt, y_sample)
```

```python
import math
import numpy as np
import concourse.bass as bass
import concourse.mybir as mybir
from concourse.bass_utils import run_bass_kernel_spmd

F32 = mybir.dt.float32
BF16 = mybir.dt.bfloat16
AF = mybir.ActivationFunctionType
ALU = mybir.AluOpType

D = 1024
FF = 2816
NH = 8
KW = 31
PLE = 256
LN_EPS = 1e-5
ALPHA = 8 ** 0.25
TT = 512
NFC = FF // 128
VPL = 336
V_LN1G, V_LN1B, V_CDWB, V_CLNG, V_CLNB, V_LN2G, V_LN2B, V_LN3G, V_LN3B, V_SUBG, V_CDW = 0, 8, 16, 24, 32, 40, 48, 56, 64, 72, 80
WSLOT = 5632
NSLOT = 4


class Buf:
    __slots__ = ("w", "rd", "rdd")

    def __init__(self):
        self.w = None
        self.rd = {}
        self.rdd = []


class Op:
    __slots__ = ("eng", "fn", "deps", "sig", "sem", "cnt", "isdma", "prev")

    def __init__(self, eng, fn, isdma):
        self.eng = eng
        self.fn = fn
        self.isdma = isdma
        self.deps = []
        self.sig = isdma
        self.sem = None
        self.cnt = 0
        self.prev = 0


class Prog:
    ENG = ("pe", "act", "dve", "pool", "sp")

    def __init__(self, ndsem=12):
        self.q = {e: [] for e in self.ENG}
        self.last = {e: None for e in self.ENG}
        self.ndsem = ndsem
        self.dcount = {}
        self.dlast = {}
        self.drot = {"sp": 0, "pool": 0}

    def op(self, eng, fn, rd=(), wr=(), dma=False):
        o = Op(eng, fn, dma)
        deps = {}
        rawset = set()
        for b in rd:
            if b.w is not None:
                deps[id(b.w)] = b.w
                rawset.add(id(b.w))
        for b in wr:
            if b.w is not None:
                deps[id(b.w)] = b.w
            for r in b.rd.values():
                deps[id(r)] = r
            for r in b.rdd:
                deps[id(r)] = r
        for k, d in deps.items():
            if d is o:
                continue
            if d.eng == eng and not d.isdma and not dma:
                if eng == "pe":
                    continue
                if k not in rawset:
                    continue
            o.deps.append(d)
            d.sig = True
        for b in rd:
            if dma:
                b.rdd.append(o)
            else:
                b.rd[eng] = o
        for b in wr:
            b.w = o
            b.rd = {}
            b.rdd = []
        if dma:
            k = self.drot[eng]
            self.drot[eng] = (k + 1) % self.ndsem
            key = (eng, k)
            n = self.dcount.get(key, 0)
            o.sem = key
            o.prev = 16 * n
            o.cnt = 16 * (n + 1)
            self.dcount[key] = n + 1
            self.dlast[key] = o
        self.q[eng].append(o)
        if not dma:
            self.last[eng] = o
        return o

    def barrier(self):
        lasts = dict(self.last)
        dl = list(self.dlast.values())
        for e in self.ENG:
            o = Op(e, None, False)
            for e2, lo in lasts.items():
                if lo is not None and e2 != e:
                    o.deps.append(lo)
                    lo.sig = True
            o.deps.extend(dl)
            self.q[e].append(o)

    def emit(self, nc, block, csem, dsem):
        for e in self.ENG:
            n = 0
            for o in self.q[e]:
                if o.isdma or o.fn is None:
                    continue
                if o.sig:
                    n += 1
                    o.sem = e
                    o.cnt = n

        def handle(o):
            return dsem[o.sem] if o.isdma else csem[o.sem]

        def run(e, eng):
            waited = {}
            for o in self.q[e]:
                for d in o.deps:
                    if waited.get(d.sem, 0) < d.cnt:
                        eng.wait_ge(handle(d), d.cnt)
                        waited[d.sem] = d.cnt
                if o.fn is None:
                    continue
                if o.isdma and o.prev > 0 and waited.get(o.sem, 0) < o.prev:
                    eng.wait_ge(dsem[o.sem], o.prev)
                    waited[o.sem] = o.prev
                ins = o.fn(eng)
                if o.sig:
                    ins.then_inc(handle(o), 16 if o.isdma else 1)

        @block.tensor
        def _(eng):
            run("pe", eng)

        @block.scalar
        def _(eng):
            run("act", eng)

        @block.vector
        def _(eng):
            run("dve", eng)

        @block.gpsimd
        def _(eng):
            run("pool", eng)

        @block.sync
        def _(eng):
            run("sp", eng)


def build_program(L, S0, S1):
    NTOK = S0 + 2 * S1
    SMAX = max(S0, S1)
    seqs = [(0, S0), (S0, S1), (S0 + S1, S1)]
    tiles = []
    for (st, ln) in seqs:
        for t0 in range(0, ln, TT):
            tiles.append((st + t0, t0, st, ln))
    NT = len(tiles)

    nc = bass.Bass("TRN2", target_bir_lowering=False)
    P = Prog()

    def din(name, shape, dt=F32):
        return nc.dram_tensor(name, list(shape), dt, kind="ExternalInput").ap()

    def dint(name, shape, dt=BF16):
        return nc.dram_tensor(name, list(shape), dt, kind="Internal").ap()

    x_d = din("x", [NTOK, D])
    p_d = din("p", [L, NTOK, PLE])
    vecs_d = din("vecs", [128, L * VPL])
    lam_d = din("lamT", [64, 4 * L])
    kc_d = din("kc", [128, 2 * L])
    ident_d = din("ident", [128, 128])
    cos_d = din("cos", [128, SMAX])
    sin_d = din("sin", [128, SMAX])
    wnames = [("gu1", 11, 4096), ("d1", 4, 5632), ("win", 18, 4096), ("pw2", 2, 4096), ("wo", 2, 4096),
              ("pg", 2, 4096), ("pp", 1, 2048), ("gu2", 11, 4096), ("d2", 4, 5632)]
    wf = {}
    wb = {}
    wbuf = {}
    for (nm, nb, sz) in wnames:
        wf[nm] = din("w_" + nm, [L * nb, 128, sz])
        wb[nm] = dint("wb_" + nm, [L * nb, 128, sz])
        wbuf[nm] = [Buf() for _ in range(L * nb)]
    wb["cv"] = dint("wb_cv", [L * 8, 128, KW * 128])
    wbuf["cv"] = [Buf() for _ in range(L * 8)]
    y_d = nc.dram_tensor("y", [NTOK, D], F32, kind="ExternalOutput").ap()

    x1s = dint("x1s", [8, 128, NTOK], F32)
    qs = dint("qs", [8, 128, NTOK])
    ks = dint("ks", [8, 128, NTOK])
    vs = dint("vs", [NTOK, D])
    cs = dint("cs", [2, 8, 128, NTOK])
    gs = dint("gs", [16, 128, NTOK])
    ats = dint("ats", [8, 128, NTOK])

    base = (nc.sbuf_base + 63) // 64 * 64
    top = nc.sbuf_top
    cur = [base]

    def sb(name, shape, dt):
        nbytes = int(np.prod(shape[1:])) * (4 if dt == F32 else 2)
        off = cur[0]
        cur[0] = (off + nbytes + 63) // 64 * 64
        assert cur[0] <= top, f"SBUF overflow at {name}: {cur[0]} > {top}"
        return nc.alloc_sbuf_tensor_at(name, list(shape), dt, offset=off)

    vecs = sb("vecs", [128, L * VPL], F32)
    identf = sb("identf", [128, 128], F32)
    identb = sb("identb", [128, 128], BF16)
    onesb = sb("onesb", [128, 128], BF16)
    onesf = sb("onesf", [128, 128], F32)
    kct = sb("kct", [128, 2 * L], F32)
    lamt = sb("lamt", [128, 4 * L], F32)
    epst = sb("epst", [128, 4], F32)
    wsl = [sb(f"wsl{i}", [128, WSLOT], BF16) for i in range(NSLOT)]
    wslb = [Buf() for _ in range(NSLOT)]
    cbuf = Buf()
    phase_base = cur[0]

    psw = [nc.alloc_psum_tensor(f"psw{i}", [128, 1024], F32) for i in range(4)]
    psb = [psw[i // 2][:, (i % 2) * 512:(i % 2 + 1) * 512] for i in range(8)]
    psbuf = [Buf() for _ in range(8)]

    def vcol(l, off, c=0):
        k = l * VPL + off + c
        return vecs[:, k:k + 1]

    def mm_group(out_ap, pairs, rd, wr, start=True, stop=True):
        n = len(pairs)

        def fn(e):
            ins = None
            for i, (lt, rh) in enumerate(pairs):
                ins = e.matmul(out_ap, lhsT=lt, rhs=rh, start=(start and i == 0), stop=(stop and i == n - 1))
            return ins
        return P.op("pe", fn, rd=rd, wr=wr)

    def act(out, in_, func, rd, wr, bias=None, scale=None):
        kw = {}
        if bias is not None:
            kw["bias"] = bias
        if scale is not None:
            kw["scale"] = scale
        return P.op("act", lambda e: e.activation(out=out, in_=in_, func=func, **kw), rd=rd, wr=wr)

    def tt(out, in0, in1, op, rd, wr, eng="dve"):
        return P.op(eng, lambda e: e.tensor_tensor(out=out, in0=in0, in1=in1, op=op), rd=rd, wr=wr)

    def ts(out, in0, s1, op0, rd, wr, s2=None, op1=None, eng="dve"):
        if op1 is None:
            return P.op(eng, lambda e: e.tensor_scalar(out=out, in0=in0, scalar1=s1, scalar2=None, op0=op0), rd=rd, wr=wr)
        return P.op(eng, lambda e: e.tensor_scalar(out=out, in0=in0, scalar1=s1, scalar2=s2, op0=op0, op1=op1), rd=rd, wr=wr)

    def stt(out, in0, scalar, in1, op0, op1, rd, wr):
        return P.op("dve", lambda e: e.scalar_tensor_tensor(out=out, in0=in0, scalar=scalar, in1=in1, op0=op0, op1=op1),
                    rd=rd, wr=wr)

    def dma(q, out, in_, rd, wr):
        return P.op(q, lambda e: e.dma_start(out=out, in_=in_), rd=rd, wr=wr, dma=True)

    wctr = [0]

    def wget(nm, idx, sz):
        k = wctr[0] % NSLOT
        wctr[0] += 1
        dma("pool", wsl[k][:, 0:sz], wb[nm][idx], rd=[wbuf[nm][idx]], wr=[wslb[k]])
        return wsl[k], wslb[k]

    import os
    KSTOP = int(os.environ.get("KSTOP", "99"))

    class _Stop(Exception):
        pass

    def chk(n):
        if KSTOP == n:
            P.barrier()
            raise _Stop()

    def body():
        cur[0] = phase_base
        dstage = [sb(f"dstage{i}", [128, KW * 128], BF16) for i in range(2)]
        dstb = [Buf(), Buf()]
        lq = sb("lq", [64, 4 * L], F32)
        lpr = sb("lpr", [64, 2 * L], F32)
        setup_b = Buf()

        cb_id = Buf()
        cb_kc = Buf()
        dma("sp", vecs[:], vecs_d, rd=[], wr=[cbuf])
        dma("sp", identf[:], ident_d, rd=[], wr=[cb_id])
        dma("sp", kct[:], kc_d, rd=[], wr=[cb_kc])
        dma("sp", lq[:], lam_d, rd=[], wr=[setup_b])
        chk(0)
        for (nm, nb, sz) in wnames:
            for i in range(L * nb):
                dma("pool", wb[nm][i], wf[nm][i], rd=[], wr=[wbuf[nm][i]])
        chk(1)
        cb2 = Buf()
        P.op("dve", lambda e: e.tensor_copy(out=identb[:], in_=identf[:]), rd=[cb_id], wr=[cb2])
        P.op("dve", lambda e: e.memset(onesb[:], 1.0), rd=[], wr=[cb2])
        P.op("dve", lambda e: e.memset(onesf[:], 1.0), rd=[], wr=[cb2])
        P.op("dve", lambda e: e.memset(epst[:, 0:1], LN_EPS / (ALPHA * ALPHA)), rd=[], wr=[cb2])
        P.op("dve", lambda e: e.memset(epst[:, 1:2], LN_EPS), rd=[], wr=[cb2])
        for l in range(L):
            for c in range(8):
                k = (l * 8 + c) % 2

                def fn(e, l=l, c=c, k=k):
                    ins = None
                    for j in range(KW):
                        ins = e.tensor_scalar(out=dstage[k][:, j * 128:(j + 1) * 128], in0=identb[:],
                                              scalar1=vcol(l, V_CDW, j * 8 + c), scalar2=None, op0=ALU.mult)
                    return ins
                P.op("dve", fn, rd=[cbuf, cb2], wr=[dstb[k]])
                dma("sp", wb["cv"][l * 8 + c], dstage[k][:], rd=[dstb[k]], wr=[wbuf["cv"][l * 8 + c]])
        chk(2)
        P.op("dve", lambda e: e.tensor_tensor(out=lpr[:, 0:L], in0=lq[:, 0:L], in1=lq[:, L:2 * L], op=ALU.mult),
             rd=[setup_b], wr=[setup_b])
        P.op("dve", lambda e: e.tensor_tensor(out=lpr[:, L:2 * L], in0=lq[:, 2 * L:3 * L], in1=lq[:, 3 * L:4 * L], op=ALU.mult),
             rd=[setup_b], wr=[setup_b])
        mm_group(psb[0][:, 0:2 * L], [(onesf[0:64, :], lpr[:, :])], rd=[setup_b, cb2], wr=[psbuf[0]])
        act(lamt[:, 2 * L:4 * L], psb[0][:, 0:2 * L], AF.Exp, rd=[psbuf[0]], wr=[setup_b])
        tt(lamt[:, 0:L], lamt[:, 3 * L:4 * L], lamt[:, 2 * L:3 * L], ALU.subtract, rd=[setup_b], wr=[setup_b])
        tt(lamt[:, 0:L], lamt[:, 0:L], kct[:, 0:L], ALU.subtract, rd=[setup_b, cb_kc], wr=[setup_b])
        for l in range(L):
            tt(lamt[:, L + l:L + l + 1], vcol(l, V_SUBG), kct[:, L + l:L + l + 1], ALU.mult, rd=[cbuf, cb_kc, setup_b], wr=[setup_b])
        chk(3)
        P.barrier()

        cur[0] = phase_base
        xf = sb("xf", [128, 8, TT], F32)
        xb = sb("xb", [128, 8, TT], BF16)
        tf = sb("tf", [128, 8, TT], F32)
        hb = sb("hb", [128, NFC, TT], BF16)
        plef = sb("plef", [128, 8, TT], F32)
        gts = sb("gts", [128, 16, TT], BF16)
        att = sb("att", [128, 8, TT], BF16)
        chalo = sb("chalo", [128, 8, TT + 32], BF16)
        stg = sb("stg", [128, 4, TT], BF16)
        stat = sb("stat", [128, 5, TT], F32)
        tmpf = sb("tmpf", [128, 4, TT], F32)
        cosb = sb("cosb", [128, 2, TT], F32)
        sinb = sb("sinb", [128, 2, TT], F32)
        oring = sb("oring", [128, 6, TT], BF16)
        pstage = sb("pstage", [128, 4, PLE], F32)
        ptb = sb("ptb", [128, 2, TT], BF16)
        ac_end = cur[0]

        xfb = [Buf() for _ in range(8)]
        xbb = [Buf() for _ in range(8)]
        tfb = [Buf() for _ in range(8)]
        hbb = [Buf() for _ in range(NFC)]
        plb = [Buf() for _ in range(8)]
        gtb = [Buf() for _ in range(16)]
        atb = [Buf() for _ in range(8)]
        chb = [Buf() for _ in range(8)]
        stgb = [Buf() for _ in range(4)]
        statb = [Buf() for _ in range(5)]
        tmpb = [Buf() for _ in range(4)]
        csb = [Buf(), Buf()]
        snb = [Buf(), Buf()]
        orb = [Buf() for _ in range(6)]
        pstb = Buf()
        ptbb = [Buf(), Buf()]

        rot = {"bank": 0, "tmp": 0, "or": 0, "stg": 0}

        def bank():
            k = rot["bank"]
            rot["bank"] = (k + 1) % 6
            return psb[k], psbuf[k]

        def tmp():
            k = rot["tmp"]
            rot["tmp"] = (k + 1) % 4
            return tmpf[:, k, :], tmpb[k]

        def oring_next():
            k = rot["or"]
            rot["or"] = (k + 1) % 6
            return oring[:, k, :], orb[k]

        S1p, S1b, S2p, S2b = psb[6], psbuf[6], psb[7], psbuf[7]

        pend = []

        def stats_pe(k, first, last):
            mm_group(S1p[:], [(onesb[:], stg[:, k, :])], rd=[stgb[k]], wr=[S1b], start=first, stop=last)
            mm_group(S2p[:], [(onesb[:], stg[:, 2 + k, :])], rd=[stgb[2 + k]], wr=[S2b], start=first, stop=last)

        def stats_flush():
            while pend:
                stats_pe(*pend.pop(0))

        def ln_stats_chunk(dc, first, last):
            k = rot["stg"]
            rot["stg"] = (k + 1) % 2
            act(stg[:, k, :], tf[:, dc, :], AF.Copy, rd=[tfb[dc]], wr=[stgb[k]])
            act(stg[:, 2 + k, :], tf[:, dc, :], AF.Square, rd=[tfb[dc]], wr=[stgb[2 + k]])
            pend.append((k, first, last))

        def stats_prev():
            while len(pend) > 1:
                stats_pe(*pend.pop(0))

        def ln_finish(l, goff, boff, epscol, mode):
            stats_flush()
            mean, msq, var, lnv, rstd = [stat[:, i, :] for i in range(5)]
            ts(mean, S1p[:], 1.0 / D, ALU.mult, rd=[S1b], wr=[statb[0]])
            tt(msq, mean, mean, ALU.mult, rd=[statb[0]], wr=[statb[1]])
            stt(var, S2p[:], 1.0 / D, msq, ALU.mult, ALU.subtract, rd=[S2b, statb[1]], wr=[statb[2]])
            act(lnv, var, AF.Ln, rd=[statb[2]], wr=[statb[3]], bias=epst[:, epscol:epscol + 1])
            act(rstd, lnv, AF.Exp, rd=[statb[3]], wr=[statb[4]], scale=-0.5)
            for dc in range(8):
                eng = "dve"
                tt(tf[:, dc, :], tf[:, dc, :], mean, ALU.subtract, rd=[tfb[dc], statb[0]], wr=[tfb[dc]], eng=eng)
                tt(tf[:, dc, :], tf[:, dc, :], rstd, ALU.mult, rd=[tfb[dc], statb[4]], wr=[tfb[dc]], eng=eng)
                g = vcol(l, goff, dc)
                b = vcol(l, boff, dc)
                if mode == "x":
                    act(xb[:, dc, :], tf[:, dc, :], AF.Identity, rd=[tfb[dc]], wr=[xbb[dc]], bias=b, scale=g)
                else:
                    act(hb[:, dc, :], tf[:, dc, :], AF.Silu, rd=[tfb[dc]], wr=[hbb[dc]], bias=b, scale=g)
            if mode == "x":
                for dc in range(8):
                    act(xf[:, dc, :], tf[:, dc, :], AF.Identity, rd=[tfb[dc]], wr=[xfb[dc]],
                        bias=vcol(l, boff, dc), scale=vcol(l, goff, dc))

        def ffn(l, which, ple):
            gu = "gu1" if which == 1 else "gu2"
            dd = "d1" if which == 1 else "d2"
            for j in range(11):
                W, Wb = wget(gu, l * 11 + j, 4096)
                Wv = W[:, 0:4096].rearrange("p (k n) -> p k n", k=8)
                for s in range(2):
                    fc = 2 * j + s
                    pg, pgb = bank()
                    pu, pub = bank()
                    mm_group(pg[:], [(Wv[:, dc, s * 128:(s + 1) * 128], xb[:, dc, :]) for dc in range(8)], rd=[Wb] + xbb, wr=[pgb])
                    mm_group(pu[:], [(Wv[:, dc, 256 + s * 128:256 + (s + 1) * 128], xb[:, dc, :]) for dc in range(8)],
                             rd=[Wb] + xbb, wr=[pub])
                    t, tb_ = tmp()
                    act(t, pg[:], AF.Silu, rd=[pgb], wr=[tb_])
                    tt(hb[:, fc, :], t, pu[:], ALU.mult, rd=[tb_, pub], wr=[hbb[fc]])
            for j in range(4):
                W, Wb = wget(dd, l * 4 + j, 5632)
                Wv = W[:, 0:5632].rearrange("p (k n) -> p k n", k=NFC)
                for s in range(2):
                    dc = 2 * j + s
                    po, pob = bank()
                    mm_group(po[:], [(Wv[:, fc, s * 128:(s + 1) * 128], hb[:, fc, :]) for fc in range(NFC)], rd=[Wb] + hbb, wr=[pob])
                    stats_prev()
                    stt(tf[:, dc, :], po[:], 0.5 / ALPHA, xf[:, dc, :], ALU.mult, ALU.add, rd=[pob, xfb[dc]], wr=[tfb[dc]])
                    if ple:
                        stt(tf[:, dc, :], plef[:, dc, :], 1.0 / ALPHA, tf[:, dc, :], ALU.mult, ALU.add,
                            rd=[plb[dc], tfb[dc]], wr=[tfb[dc]])
                    ln_stats_chunk(dc, dc == 0, dc == 7)

        def load_x_tile(i):
            tok0 = tiles[i][0]
            xsv = plef[:].rearrange("p c t -> p (c t)").rearrange("p (s f) -> p s f", s=4)
            dma("sp", xsv, x_d[tok0:tok0 + TT, :].rearrange("(s p) f -> p s f", p=128), rd=[], wr=plb)

        def input_transpose(i):
            tfv = plef[:].rearrange("p c t -> p (c t)").rearrange("p (s f) -> p s f", s=4)
            for dc in range(8):
                pb_, pbb = bank()

                def fn(e, dc=dc, pb_=pb_):
                    ins = None
                    for s in range(4):
                        ins = e.transpose(pb_[:, s * 128:(s + 1) * 128], tfv[:, s, dc * 128:(dc + 1) * 128], identf[:])
                    return ins
                P.op("pe", fn, rd=plb, wr=[pbb])
                chk(32)
                act(xf[:, dc, :], pb_[:], AF.Copy, rd=[pbb], wr=[xfb[dc]])
                chk(33)
                P.op("dve", lambda e, dc=dc: e.tensor_copy(out=xb[:, dc, :], in_=xf[:, dc, :]), rd=[xfb[dc]], wr=[xbb[dc]])
                chk(34 + dc)

        def output_store(i):
            tok0 = tiles[i][0]
            tfv = tf[:].rearrange("p c t -> p (c t)").rearrange("p (s f) -> p s f", s=4)
            for s in range(4):
                for half in range(2):
                    pb_, pbb = bank()

                    def fn(e, s=s, half=half, pb_=pb_):
                        ins = None
                        for q4 in range(4):
                            dc = half * 4 + q4
                            ins = e.transpose(pb_[:, q4 * 128:(q4 + 1) * 128], xf[:, dc, s * 128:(s + 1) * 128], identf[:])
                        return ins
                    P.op("pe", fn, rd=xfb, wr=[pbb])
                    if half == 0:
                        act(tfv[:, s, 0:512], pb_[:], AF.Copy, rd=[pbb], wr=tfb)
                    else:
                        P.op("dve", lambda e, s=s, pb_=pb_: e.tensor_copy(out=tfv[:, s, 512:1024], in_=pb_[:]), rd=[pbb], wr=tfb)
            dma("sp", y_d[tok0:tok0 + TT, :].rearrange("(s p) f -> p s f", p=128), tfv, rd=tfb, wr=[])

        def win_stage(l, i):
            tok0, pos0, st, ln = tiles[i]
            par = l % 2
            dma("sp", x1s[:, :, tok0:tok0 + TT].rearrange("c p t -> p c t"), xf[:], rd=xfb, wr=[])
            ck = i % 2
            dma("sp", cosb[:, ck, :], cos_d[:, pos0:pos0 + TT], rd=[], wr=[csb[ck]])
            dma("sp", sinb[:, ck, :], sin_d[:, pos0:pos0 + TT], rd=[], wr=[snb[ck]])
            for qk in range(2):
                dst = qs if qk == 0 else ks
                for hp in range(4):
                    W, Wb = wget("win", l * 18 + qk * 4 + hp, 4096)
                    Wv = W[:, 0:4096].rearrange("p (k n) -> p k n", k=8)
                    for s in range(2):
                        h = 2 * hp + s
                        pz, pzb = bank()
                        pw, pwb = bank()
                        mm_group(pz[:], [(Wv[:, dc, s * 256:s * 256 + 128], xb[:, dc, :]) for dc in range(8)], rd=[Wb] + xbb, wr=[pzb])
                        mm_group(pw[:], [(Wv[:, dc, s * 256 + 128:s * 256 + 256], xb[:, dc, :]) for dc in range(8)],
                                 rd=[Wb] + xbb, wr=[pwb])
                        a, ab = tmp()
                        tt(a, pz[:], cosb[:, ck, :], ALU.mult, rd=[pzb, csb[ck]], wr=[ab])
                        b2, bb = tmp()
                        tt(b2, pw[:], sinb[:, ck, :], ALU.mult, rd=[pwb, snb[ck]], wr=[bb])
                        o, ob = oring_next()
                        tt(o, a, b2, ALU.add, rd=[ab, bb], wr=[ob])
                        dma("sp", dst[h][:, tok0:tok0 + TT], o, rd=[ob], wr=[])
            for cp in range(4):
                W, Wb = wget("win", l * 18 + 8 + cp, 4096)
                Wv = W[:, 0:4096].rearrange("p (k n) -> p k n", k=8)
                for s in range(2):
                    c = 2 * cp + s
                    pa, pab = bank()
                    pb_, pbb = bank()
                    mm_group(pa[:], [(Wv[:, dc, s * 256:s * 256 + 128], xb[:, dc, :]) for dc in range(8)], rd=[Wb] + xbb, wr=[pab])
                    mm_group(pb_[:], [(Wv[:, dc, s * 256 + 128:s * 256 + 256], xb[:, dc, :]) for dc in range(8)],
                             rd=[Wb] + xbb, wr=[pbb])
                    sg, sgb = tmp()
                    act(sg, pb_[:], AF.Sigmoid, rd=[pbb], wr=[sgb])
                    o, ob = oring_next()
                    tt(o, pa[:], sg, ALU.mult, rd=[pab, sgb], wr=[ob])
                    dma("sp", cs[par][c][:, tok0:tok0 + TT], o, rd=[ob], wr=[])
            for gb in range(4):
                W, Wb = wget("win", l * 18 + 12 + gb, 4096)
                Wv = W[:, 0:4096].rearrange("p (k n) -> p k n", k=8)
                for s in range(4):
                    ch = gb * 4 + s
                    pz, pzb = bank()
                    mm_group(pz[:], [(Wv[:, dc, s * 128:(s + 1) * 128], xb[:, dc, :]) for dc in range(8)], rd=[Wb] + xbb, wr=[pzb])
                    o, ob = oring_next()
                    act(o, pz[:], AF.Sigmoid, rd=[pzb], wr=[ob])
                    dma("sp", gs[ch][:, tok0:tok0 + TT], o, rd=[ob], wr=[])
            for vb in range(2):
                W, Wb = wget("win", l * 18 + 16 + vb, 4096)
                Wv = W[:, 0:4096].rearrange("p (k n) -> p k n", k=8)
                for s in range(4):
                    pv, pvb = bank()
                    mm_group(pv[:], [(xb[:, dc, s * 128:(s + 1) * 128], Wv[:, dc, :]) for dc in range(8)], rd=[Wb] + xbb, wr=[pvb])
                    o, ob = oring_next()
                    if s % 2 == 0:
                        act(o, pv[:], AF.Copy, rd=[pvb], wr=[ob])
                    else:
                        P.op("dve", lambda e, o=o, pv=pv: e.tensor_copy(out=o, in_=pv[:]), rd=[pvb], wr=[ob])
                    dma("sp", vs[tok0 + s * 128:tok0 + (s + 1) * 128, vb * 512:(vb + 1) * 512], o, rd=[ob], wr=[])

        def load_chalo(l, i):
            tok0, pos0, st, ln = tiles[i]
            par = l % 2
            lo = 16 if pos0 == 0 else 0
            hi = TT + 16 if pos0 + TT >= ln else TT + 32
            if lo > 0:
                P.op("dve", lambda e: e.memset(chalo[:, :, 0:16], 0.0), rd=[], wr=chb)
            if hi < TT + 32:
                P.op("dve", lambda e: e.memset(chalo[:, :, TT + 16:TT + 32], 0.0), rd=[], wr=chb)
            g0 = tok0 - 16 + lo
            dma("sp", chalo[:, :, lo:hi], cs[par][:, :, g0:g0 + (hi - lo)].rearrange("c p t -> p c t"), rd=[], wr=chb)

        def load_gts_att(i):
            tok0 = tiles[i][0]
            dma("sp", att[:], ats[:, :, tok0:tok0 + TT].rearrange("c p t -> p c t"), rd=[], wr=atb)
            dma("sp", gts[:], gs[:, :, tok0:tok0 + TT].rearrange("c p t -> p c t"), rd=[], wr=gtb)

        def load_p(l, i):
            tok0 = tiles[i][0]
            dma("sp", pstage[:], p_d[l, tok0:tok0 + TT, :].rearrange("(s p) f -> p s f", p=128), rd=[], wr=[pstb])

        def load_x1(i):
            tok0 = tiles[i][0]
            dma("sp", xf[:], x1s[:, :, tok0:tok0 + TT].rearrange("c p t -> p c t"), rd=[], wr=xfb)

        def phase_c_tile(l, i, nxt):
            for c in range(8):
                W, Wb = wget("cv", l * 8 + c, KW * 128)
                pc, pcb = bank()
                mm_group(pc[:], [(W[:, j * 128:(j + 1) * 128], chalo[:, c, 1 + j:1 + j + TT]) for j in range(KW)],
                         rd=[Wb, chb[c]], wr=[pcb])
                stats_prev()
                act(tf[:, c, :], pc[:], AF.Identity, rd=[pcb], wr=[tfb[c]], bias=vcol(l, V_CDWB, c))
                ln_stats_chunk(c, c == 0, c == 7)
            if nxt is not None:
                load_chalo(l, nxt)
            ln_finish(l, V_CLNG, V_CLNB, 1, "silu")
            for nb in range(2):
                W, Wb = wget("pw2", l * 2 + nb, 4096)
                Wv = W[:, 0:4096].rearrange("p (k n) -> p k n", k=8)
                for s in range(4):
                    dc = nb * 4 + s
                    po, pob = bank()
                    mm_group(po[:], [(Wv[:, cc, s * 128:(s + 1) * 128], hb[:, cc, :]) for cc in range(8)], rd=[Wb] + hbb[0:8], wr=[pob])
                    t1, t1b = tmp()
                    tt(t1, po[:], gts[:, 8 + dc, :], ALU.mult, rd=[pob, gtb[8 + dc]], wr=[t1b])
                    t2, t2b = tmp()
                    tt(t2, att[:, dc, :], gts[:, dc, :], ALU.mult, rd=[atb[dc], gtb[dc]], wr=[t2b])
                    tt(hb[:, 8 + dc, :], t1, t2, ALU.add, rd=[t1b, t2b], wr=[hbb[8 + dc]])
            if nxt is not None:
                load_gts_att(nxt)
            for nb in range(2):
                W, Wb = wget("wo", l * 2 + nb, 4096)
                Wv = W[:, 0:4096].rearrange("p (k n) -> p k n", k=8)
                for s in range(4):
                    dc = nb * 4 + s
                    po, pob = bank()
                    mm_group(po[:], [(Wv[:, mc, s * 128:(s + 1) * 128], hb[:, 8 + mc, :]) for mc in range(8)],
                             rd=[Wb] + hbb[8:16], wr=[pob])
                    stats_prev()
                    stt(tf[:, dc, :], po[:], 1.0 / ALPHA, xf[:, dc, :], ALU.mult, ALU.add, rd=[pob, xfb[dc]], wr=[tfb[dc]])
                    ln_stats_chunk(dc, dc == 0, dc == 7)
            ln_finish(l, V_LN2G, V_LN2B, 0, "x")
            for pc2 in range(2):
                pb_, pbb = bank()

                def fn(e, pc2=pc2, pb_=pb_):
                    ins = None
                    for s in range(4):
                        ins = e.transpose(pb_[:, s * 128:(s + 1) * 128], pstage[:, s, pc2 * 128:(pc2 + 1) * 128], identf[:])
                    return ins
                P.op("pe", fn, rd=[pstb], wr=[pbb])
                act(ptb[:, pc2, :], pb_[:], AF.Copy, rd=[pbb], wr=[ptbb[pc2]])
            if nxt is not None:
                load_p(l, nxt)
            Wp, Wpb = wget("pp", l, 2048)
            Wpv = Wp[:, 0:2048].rearrange("p (k n) -> p k n", k=2)
            for nb in range(2):
                W, Wb = wget("pg", l * 2 + nb, 4096)
                Wv = W[:, 0:4096].rearrange("p (k n) -> p k n", k=8)
                for s in range(4):
                    dc = nb * 4 + s
                    pj, pjb = bank()
                    pgt, pgtb = bank()
                    mm_group(pj[:], [(Wpv[:, k2, dc * 128:(dc + 1) * 128], ptb[:, k2, :]) for k2 in range(2)], rd=[Wpb] + ptbb, wr=[pjb])
                    mm_group(pgt[:], [(Wv[:, cc, s * 128:(s + 1) * 128], xb[:, cc, :]) for cc in range(8)], rd=[Wb] + xbb, wr=[pgtb])
                    sg, sgb = tmp()
                    act(sg, pgt[:], AF.Sigmoid, rd=[pgtb], wr=[sgb])
                    tt(plef[:, dc, :], pj[:], sg, ALU.mult, rd=[pjb, sgb], wr=[plb[dc]])
            ffn(l, 2, True)
            ln_finish(l, V_LN3G, V_LN3B, 0, "x")

        load_x_tile(0)
        chk(31)
        for i in range(NT):
            input_transpose(i)
            chk(4)
            if i + 1 < NT:
                load_x_tile(i + 1)
            ffn(0, 1, False)
            chk(5)
            ln_finish(0, V_LN1G, V_LN1B, 0, "x")
            chk(6)
            win_stage(0, i)
            chk(7)
        P.barrier()
        chk(8)

        cur[0] = phase_base
        kT = [sb(f"kT{i}", [128, SMAX], BF16) for i in range(2)]
        vv = [sb(f"vv{i}", [128, SMAX // 128, 128], BF16) for i in range(2)]
        qt = [sb(f"qt{i}", [128, TT], BF16) for i in range(2)]
        pT = sb("pT", [128, 6, TT], BF16)
        accd = sb("accd", [128, 2, TT], F32)
        accp = sb("accp", [128, 2, TT], F32)
        nrm = sb("nrm", [128, 9, TT], F32)
        osq = sb("osq", [128, TT], BF16)
        ores = sb("ores", [128, 2, TT], BF16)
        kvb = [Buf(), Buf()]
        vvb = [Buf(), Buf()]
        qtb = [Buf(), Buf()]
        pTb = [Buf() for _ in range(6)]
        accb_ = {("dve", 0): Buf(), ("dve", 1): Buf(), ("pool", 0): Buf(), ("pool", 1): Buf()}
        nrb = [Buf() for _ in range(9)]
        osqb = Buf()
        oresb = [Buf(), Buf()]

        def attention(l):
            units = []
            job = 0
            for (st, ln) in seqs:
                for h in range(NH):
                    for qi in range(ln // TT):
                        units.append((job, st, ln, h, qi))
                    job += 1

            def load_kv(u):
                jb, st, ln, h, qi = u
                k = jb % 2
                dma("sp", kT[k][:, 0:ln], ks[h][:, st:st + ln], rd=[], wr=[kvb[k]])
                dma("sp", vv[k][:, 0:ln // 128, :], vs[st:st + ln, h * 128:(h + 1) * 128].rearrange("(c p) e -> p c e", p=128),
                    rd=[], wr=[vvb[k]])

            def load_q(ui):
                jb, st, ln, h, qi = units[ui]
                dma("sp", qt[ui % 2][:], qs[h][:, st + qi * TT:st + (qi + 1) * TT], rd=[], wr=[qtb[ui % 2]])

            load_kv(units[0])
            load_q(0)
            prot = [0]
            for ui, u in enumerate(units):
                jb, st, ln, h, qi = u
                if ui + 1 < len(units):
                    if units[ui + 1][0] != jb:
                        load_kv(units[ui + 1])
                    load_q(ui + 1)
                kb = jb % 2
                q = qt[ui % 2]
                qb_ = qtb[ui % 2]
                nk = ln // 128
                pmap = {}
                used = {}

                def who(j, m):
                    if (j + m) % 2 == 0:
                        return "pe"
                    idx = (j - (1 - m)) // 2
                    return "pool" if idx % 4 == 3 else "dve"

                def QK(j):
                    for m in range(2):
                        bk = 4 + (j % 2) * 2 + m
                        mm_group(psb[bk][:], [(kT[kb][64 * m:64 * m + 64, j * 128:(j + 1) * 128], q[64 * m:64 * m + 64, :])],
                                 rd=[kvb[kb], qb_], wr=[psbuf[bk]])

                def EXP(j):
                    for m in range(2):
                        bk = 4 + (j % 2) * 2 + m
                        r = prot[0]
                        prot[0] = (r + 1) % 6
                        pmap[(j, m)] = r
                        act(pT[:, r, :], psb[bk][:], AF.Exp, rd=[psbuf[bk]], wr=[pTb[r]], scale=0.125)

                def PVL(j):
                    for m in range(2):
                        r = pmap[(j, m)]
                        w = who(j, m)
                        mm_group(psb[m][:], [(vv[kb][:, j, :], pT[:, r, :])], rd=[vvb[kb], pTb[r]], wr=[psbuf[m]],
                                 start=(j == 0), stop=(j == nk - 1))
                        if w == "pe":
                            mm_group(psb[2 + m][:], [(onesb[:], pT[:, r, :])], rd=[pTb[r]], wr=[psbuf[2 + m]],
                                     start=(j == m), stop=False)
                        else:
                            acc = (accp if w == "pool" else accd)[:, m, :]
                            ab = accb_[(w, m)]
                            src = pT[:, r, :]
                            if not used.get((w, m)):
                                used[(w, m)] = True
                                P.op(w, lambda e, acc=acc, src=src: e.tensor_copy(out=acc, in_=src), rd=[pTb[r]], wr=[ab])
                            else:
                                P.op(w, lambda e, acc=acc, src=src: e.tensor_tensor(out=acc, in0=acc, in1=src, op=ALU.add),
                                     rd=[ab, pTb[r]], wr=[ab])

                QK(0)
                for j in range(nk):
                    if j + 1 < nk:
                        QK(j + 1)
                    EXP(j)
                    PVL(j)
                for m in range(2):
                    pairs = []
                    rdl = []
                    for w, accx in (("dve", accd), ("pool", accp)):
                        if used.get((w, m)):
                            pairs.append((onesf[:], accx[:, m, :]))
                            rdl.append(accb_[(w, m)])
                    mm_group(psb[2 + m][:], pairs, rd=rdl, wr=[psbuf[2 + m]], start=False, stop=True)
                N = [nrm[:, i, :] for i in range(9)]
                act(N[2], psb[0][:], AF.Copy, rd=[psbuf[0]], wr=[nrb[2]])
                P.op("dve", lambda e: e.tensor_copy(out=N[3], in_=psb[1][:]), rd=[psbuf[1]], wr=[nrb[3]])
                act(N[0], psb[2][:], AF.Copy, rd=[psbuf[2]], wr=[nrb[0]])
                P.op("dve", lambda e: e.tensor_copy(out=N[1], in_=psb[3][:]), rd=[psbuf[3]], wr=[nrb[1]])
                P.op("dve", lambda e: e.reciprocal(out=N[4], in_=N[0]), rd=[nrb[0]], wr=[nrb[4]])
                P.op("dve", lambda e: e.reciprocal(out=N[5], in_=N[1]), rd=[nrb[1]], wr=[nrb[5]])
                tt(N[6], N[2], N[4], ALU.mult, rd=[nrb[2], nrb[4]], wr=[nrb[6]])
                tt(N[7], N[3], N[5], ALU.mult, rd=[nrb[3], nrb[5]], wr=[nrb[7]])
                stt(N[6], N[7], lamt[:, l:l + 1], N[6], ALU.mult, ALU.add, rd=[nrb[7], nrb[6]], wr=[nrb[6]])
                tt(osq[:], N[6], N[6], ALU.mult, rd=[nrb[6]], wr=[osqb])
                mm_group(psb[4][:], [(onesb[:], osq[:])], rd=[osqb], wr=[psbuf[4]])
                act(N[8], psb[4][:], AF.Ln, rd=[psbuf[4]], wr=[nrb[8]], bias=epst[:, 1:2], scale=1.0 / 128)
                act(N[8], N[8], AF.Exp, rd=[nrb[8]], wr=[nrb[8]], scale=-0.5)
                ok = ui % 2
                stt(ores[:, ok, :], N[6], lamt[:, L + l:L + l + 1], N[8], ALU.mult, ALU.mult, rd=[nrb[6], nrb[8]], wr=[oresb[ok]])
                tok0 = st + qi * TT
                dma("sp", ats[h][:, tok0:tok0 + TT], ores[:, ok, :], rd=[oresb[ok]], wr=[])

        for l in range(L):
            attention(l)
            chk(9)
            P.barrier()
            load_chalo(l, 0)
            load_gts_att(0)
            load_p(l, 0)
            load_x1(0)
            for i in range(NT):
                nxt = i + 1 if i + 1 < NT else None
                phase_c_tile(l, i, nxt)
                chk(10)
                if l + 1 < L:
                    ffn(l + 1, 1, False)
                    ln_finish(l + 1, V_LN1G, V_LN1B, 0, "x")
                    win_stage(l + 1, i)
                else:
                    output_store(i)
                if nxt is not None:
                    load_x1(nxt)
            P.barrier()

    try:
        body()
    except _Stop:
        pass

    from contextlib import ExitStack
    with ExitStack() as es:
        csem = {e: es.enter_context(nc.semaphore("c_" + e)) for e in ("pe", "act", "dve", "pool")}
        dsem = {}
        for qn in ("sp", "pool"):
            for k in range(P.ndsem):
                dsem[(qn, k)] = es.enter_context(nc.semaphore(f"d_{qn}{k}"))
        block = es.enter_context(nc.Block())
        P.emit(nc, block, csem, dsem)
    return nc


def _blk(W):
    K, n = W.shape
    kc = K // 128
    return np.ascontiguousarray(W.reshape(kc, 128, n).transpose(1, 0, 2).reshape(128, kc * n))


def _c128(v):
    return np.ascontiguousarray(np.asarray(v).reshape(-1, 128).T)


def _prep_shared(inp, L, SMAX):
    f32 = np.float32
    sh = {}
    g = {k: np.asarray(v, dtype=f32) for k, v in inp.items() if k not in ("x_prompt", "x_sample", "p_prompt", "p_sample")}
    vecs = np.zeros((128, L * VPL), f32)
    for l in range(L):
        o = l * VPL
        for off, nm in ((V_LN1G, "ln1_g"), (V_LN1B, "ln1_b"), (V_CDWB, "conv_dw_b"), (V_CLNG, "conv_ln_g"),
                        (V_CLNB, "conv_ln_b"), (V_LN2G, "ln2_g"), (V_LN2B, "ln2_b"), (V_LN3G, "ln3_g"), (V_LN3B, "ln3_b")):
            vecs[:, o + off:o + off + 8] = _c128(g[nm][l])
        vecs[:, o + V_SUBG:o + V_SUBG + 1] = g["subln_g"][l].reshape(128, 1)
        vecs[:, o + V_CDW:o + V_CDW + KW * 8] = g["conv_dw"][l].reshape(KW * 8, 128).T
    sh["vecs"] = vecs
    lamT = np.zeros((64, 4 * L), f32)
    for wi, nm in enumerate(("lam_q1", "lam_k1", "lam_q2", "lam_k2")):
        lamT[:, wi * L:(wi + 1) * L] = g[nm].T
    sh["lamT"] = lamT
    kc = np.zeros((128, 2 * L), f32)
    for l in range(L):
        li = 0.8 - 0.6 * math.exp(-0.3 * l)
        kc[:, l] = li
        kc[:, L + l] = 1.0 - li
    sh["kc"] = kc
    sh["ident"] = np.eye(128, dtype=f32)
    inv = (f32(10000.0) ** (-np.arange(32, dtype=f32) / f32(32))).astype(f32)
    ang = (np.arange(SMAX, dtype=f32)[None, :] * inv[:, None]).astype(f32)
    cs_, sn_ = np.cos(ang).astype(f32), np.sin(ang).astype(f32)
    sh["cos"] = np.ascontiguousarray(np.tile(cs_, (4, 1)))
    sh["sin"] = np.ascontiguousarray(np.concatenate([-sn_, sn_, -sn_, sn_], axis=0))

    def stack(fn, nb):
        return np.ascontiguousarray(np.stack([fn(l, j) for l in range(L) for j in range(nb)]))

    for which in (1, 2):
        wg, wu, wd = g[f"ffn{which}_w_gate"], g[f"ffn{which}_w_up"], g[f"ffn{which}_w_down"]
        sh[f"w_gu{which}"] = stack(lambda l, j: _blk(np.concatenate([wg[l][:, j * 256:(j + 1) * 256],
                                                                     wu[l][:, j * 256:(j + 1) * 256]], axis=1)), 11)
        sh[f"w_d{which}"] = stack(lambda l, j: _blk(wd[l][:, j * 256:(j + 1) * 256]), 4)
    win = g["w_in"]

    def swp(h0):
        idx = []
        for m in range(2):
            b0 = h0 + m * 64
            idx += list(range(b0 + 32, b0 + 64)) + list(range(b0, b0 + 32))
        return idx

    def win_block(l, j):
        W = win[l]
        if j < 8:
            base = 0 if j < 4 else 1024
            hp = j % 4
            cols = []
            for s in range(2):
                h0 = base + (2 * hp + s) * 128
                cols += list(range(h0, h0 + 128)) + swp(h0)
        elif j < 12:
            cp = j - 8
            cols = []
            for s in range(2):
                c = 2 * cp + s
                cols += list(range(3072 + c * 128, 3072 + (c + 1) * 128)) + list(range(4096 + c * 128, 4096 + (c + 1) * 128))
        elif j < 16:
            gb = j - 12
            cols = list(range(5120 + gb * 512, 5120 + (gb + 1) * 512))
        else:
            vb = j - 16
            cols = list(range(2048 + vb * 512, 2048 + (vb + 1) * 512))
        return _blk(W[:, cols])
    sh["w_win"] = stack(win_block, 18)
    sh["w_pw2"] = stack(lambda l, j: _blk(g["conv_pw2"][l][:, j * 512:(j + 1) * 512]), 2)
    sh["w_wo"] = stack(lambda l, j: _blk(g["w_o"][l][:, j * 512:(j + 1) * 512]), 2)
    sh["w_pg"] = stack(lambda l, j: _blk(g["ple_w_gate"][l][:, j * 512:(j + 1) * 512]), 2)
    sh["w_pp"] = stack(lambda l, j: _blk(g["ple_w_proj"][l]), 1)
    return sh


def kernel(**inputs):
    xs = np.asarray(inputs["x_sample"], dtype=np.float32)
    xp = np.asarray(inputs["x_prompt"], dtype=np.float32)
    ps_ = np.asarray(inputs["p_sample"], dtype=np.float32)
    pp_ = np.asarray(inputs["p_prompt"], dtype=np.float32)
    n = xs.shape[0]
    S0 = xs.shape[1]
    S1 = xp.shape[1]
    L = ps_.shape[0]
    assert xp.shape[0] == 2 * n
    shared = _prep_shared(inputs, L, max(S0, S1))
    nc = build_program(L, S0, S1)
    in_maps = []
    for c in range(n):
        m = dict(shared)
        m["x"] = np.ascontiguousarray(np.concatenate([xs[c], xp[2 * c], xp[2 * c + 1]], axis=0))
        m["p"] = np.ascontiguousarray(np.concatenate([ps_[:, c], pp_[:, 2 * c], pp_[:, 2 * c + 1]], axis=1))
        in_maps.append(m)
    res = run_bass_kernel_spmd(nc, in_maps, core_ids=list(range(n)))
    y_s = np.empty_like(xs)
    y_p = np.empty_like(xp)
    for c in range(n):
        y = res.results[c]["y"]
        y_s[c] = y[0:S0]
        y_p[2 * c] = y[S0:S0 + S1]
        y_p[2 * c + 1] = y[S0 + S1:S0 + 2 * S1]
    return (y_p, y_s)
```

```python
import math
import numpy as np
import concourse.bass as bass
import concourse.mybir as mybir
from concourse.bass_utils import run_bass_kernel_spmd

F32 = mybir.dt.float32
BF16 = mybir.dt.bfloat16
AF = mybir.ActivationFunctionType
ALU = mybir.AluOpType

D = 1024
FF = 2816
NH = 8
KW = 31
PLE = 256
LN_EPS = 1e-5
ALPHA = 8 ** 0.25
TT = 512
NFC = FF // 128
VPL = 336
V_LN1G, V_LN1B, V_CDWB, V_CLNG, V_CLNB, V_LN2G, V_LN2B, V_LN3G, V_LN3B, V_SUBG, V_CDW = 0, 8, 16, 24, 32, 40, 48, 56, 64, 72, 80
WSLOT = 5632
NSLOT = 4


class Buf:
    __slots__ = ("w", "rd", "rdd")

    def __init__(self):
        self.w = None
        self.rd = {}
        self.rdd = []


class Op:
    __slots__ = ("eng", "fn", "deps", "sig", "sem", "cnt", "isdma", "prev")

    def __init__(self, eng, fn, isdma):
        self.eng = eng
        self.fn = fn
        self.isdma = isdma
        self.deps = []
        self.sig = isdma
        self.sem = None
        self.cnt = 0
        self.prev = 0


class Prog:
    ENG = ("pe", "act", "dve", "pool", "sp")

    def __init__(self, ndsem=12):
        self.q = {e: [] for e in self.ENG}
        self.last = {e: None for e in self.ENG}
        self.ndsem = ndsem
        self.dcount = {}
        self.dlast = {}
        self.drot = {"sp": 0, "pool": 0}

    def op(self, eng, fn, rd=(), wr=(), dma=False):
        o = Op(eng, fn, dma)
        deps = {}
        rawset = set()
        for b in rd:
            if b.w is not None:
                deps[id(b.w)] = b.w
                rawset.add(id(b.w))
        for b in wr:
            if b.w is not None:
                deps[id(b.w)] = b.w
            for r in b.rd.values():
                deps[id(r)] = r
            for r in b.rdd:
                deps[id(r)] = r
        for k, d in deps.items():
            if d is o:
                continue
            if d.eng == eng and not d.isdma and not dma:
                if eng == "pe":
                    continue
                if k not in rawset:
                    continue
            o.deps.append(d)
            d.sig = True
        for b in rd:
            if dma:
                b.rdd.append(o)
            else:
                b.rd[eng] = o
        for b in wr:
            b.w = o
            b.rd = {}
            b.rdd = []
        if dma:
            k = self.drot[eng]
            self.drot[eng] = (k + 1) % self.ndsem
            key = (eng, k)
            n = self.dcount.get(key, 0)
            o.sem = key
            o.prev = 16 * n
            o.cnt = 16 * (n + 1)
            self.dcount[key] = n + 1
            self.dlast[key] = o
        self.q[eng].append(o)
        if not dma:
            self.last[eng] = o
        return o

    def barrier(self):
        lasts = dict(self.last)
        dl = list(self.dlast.values())
        for e in self.ENG:
            o = Op(e, None, False)
            for e2, lo in lasts.items():
                if lo is not None and e2 != e:
                    o.deps.append(lo)
                    lo.sig = True
            o.deps.extend(dl)
            self.q[e].append(o)

    def emit(self, nc, block, csem, dsem):
        for e in self.ENG:
            n = 0
            for o in self.q[e]:
                if o.isdma or o.fn is None:
                    continue
                if o.sig:
                    n += 1
                    o.sem = e
                    o.cnt = n

        def handle(o):
            return dsem[o.sem] if o.isdma else csem[o.sem]

        def run(e, eng):
            waited = {}
            for o in self.q[e]:
                for d in o.deps:
                    if waited.get(d.sem, 0) < d.cnt:
                        eng.wait_ge(handle(d), d.cnt)
                        waited[d.sem] = d.cnt
                if o.fn is None:
                    continue
                if o.isdma and o.prev > 0 and waited.get(o.sem, 0) < o.prev:
                    eng.wait_ge(dsem[o.sem], o.prev)
                    waited[o.sem] = o.prev
                ins = o.fn(eng)
                if o.sig:
                    ins.then_inc(handle(o), 16 if o.isdma else 1)

        @block.tensor
        def _(eng):
            run("pe", eng)

        @block.scalar
        def _(eng):
            run("act", eng)

        @block.vector
        def _(eng):
            run("dve", eng)

        @block.gpsimd
        def _(eng):
            run("pool", eng)

        @block.sync
        def _(eng):
            run("sp", eng)


def build_program(L, S0, S1):
    NTOK = S0 + 2 * S1
    SMAX = max(S0, S1)
    seqs = [(0, S0), (S0, S1), (S0 + S1, S1)]
    tiles = []
    for (st, ln) in seqs:
        for t0 in range(0, ln, TT):
            tiles.append((st + t0, t0, st, ln))
    NT = len(tiles)

    nc = bass.Bass("TRN2", target_bir_lowering=False)
    P = Prog()

    def din(name, shape, dt=F32):
        return nc.dram_tensor(name, list(shape), dt, kind="ExternalInput").ap()

    def dint(name, shape, dt=BF16):
        return nc.dram_tensor(name, list(shape), dt, kind="Internal").ap()

    x_d = din("x", [NTOK, D])
    p_d = din("p", [L, NTOK, PLE])
    vecs_d = din("vecs", [128, L * VPL])
    lam_d = din("lamT", [64, 4 * L])
    kc_d = din("kc", [128, 2 * L])
    ident_d = din("ident", [128, 128])
    cos_d = din("cos", [128, SMAX])
    sin_d = din("sin", [128, SMAX])
    wnames = [("gu1", 11, 4096), ("d1", 4, 5632), ("win", 18, 4096), ("pw2", 2, 4096), ("wo", 2, 4096),
              ("pg", 2, 4096), ("pp", 1, 2048), ("gu2", 11, 4096), ("d2", 4, 5632)]
    wf = {}
    wb = {}
    wbuf = {}
    for (nm, nb, sz) in wnames:
        wf[nm] = din("w_" + nm, [L * nb, 128, sz])
        wb[nm] = dint("wb_" + nm, [L * nb, 128, sz])
        wbuf[nm] = [Buf() for _ in range(L * nb)]
    wb["cv"] = dint("wb_cv", [L * 8, 128, KW * 128])
    wbuf["cv"] = [Buf() for _ in range(L * 8)]
    y_d = nc.dram_tensor("y", [NTOK, D], F32, kind="ExternalOutput").ap()

    x1s = dint("x1s", [8, 128, NTOK], F32)
    qs = dint("qs", [8, 128, NTOK])
    ks = dint("ks", [8, 128, NTOK])
    vs = dint("vs", [NTOK, D])
    cs = dint("cs", [2, 8, 128, NTOK])
    gs = dint("gs", [16, 128, NTOK])
    ats = dint("ats", [8, 128, NTOK])

    base = (nc.sbuf_base + 63) // 64 * 64
    top = nc.sbuf_top
    cur = [base]

    def sb(name, shape, dt):
        nbytes = int(np.prod(shape[1:])) * (4 if dt == F32 else 2)
        off = cur[0]
        cur[0] = (off + nbytes + 63) // 64 * 64
        assert cur[0] <= top, f"SBUF overflow at {name}: {cur[0]} > {top}"
        return nc.alloc_sbuf_tensor_at(name, list(shape), dt, offset=off)

    vecs = sb("vecs", [128, L * VPL], F32)
    identf = sb("identf", [128, 128], F32)
    identb = sb("identb", [128, 128], BF16)
    onesb = sb("onesb", [128, 128], BF16)
    onesf = sb("onesf", [128, 128], F32)
    kct = sb("kct", [128, 2 * L], F32)
    lamt = sb("lamt", [128, 4 * L], F32)
    epst = sb("epst", [128, 4], F32)
    wsl = [sb(f"wsl{i}", [128, WSLOT], BF16) for i in range(NSLOT)]
    wslb = [Buf() for _ in range(NSLOT)]
    cbuf = Buf()
    phase_base = cur[0]

    psw = [nc.alloc_psum_tensor(f"psw{i}", [128, 1024], F32) for i in range(4)]
    psb = [psw[i // 2][:, (i % 2) * 512:(i % 2 + 1) * 512] for i in range(8)]
    psbuf = [Buf() for _ in range(8)]

    def vcol(l, off, c=0):
        k = l * VPL + off + c
        return vecs[:, k:k + 1]

    def mm_group(out_ap, pairs, rd, wr, start=True, stop=True):
        n = len(pairs)

        def fn(e):
            ins = None
            for i, (lt, rh) in enumerate(pairs):
                ins = e.matmul(out_ap, lhsT=lt, rhs=rh, start=(start and i == 0), stop=(stop and i == n - 1))
            return ins
        return P.op("pe", fn, rd=rd, wr=wr)

    def act(out, in_, func, rd, wr, bias=None, scale=None):
        kw = {}
        if bias is not None:
            kw["bias"] = bias
        if scale is not None:
            kw["scale"] = scale
        return P.op("act", lambda e: e.activation(out=out, in_=in_, func=func, **kw), rd=rd, wr=wr)

    def tt(out, in0, in1, op, rd, wr, eng="dve"):
        return P.op(eng, lambda e: e.tensor_tensor(out=out, in0=in0, in1=in1, op=op), rd=rd, wr=wr)

    def ts(out, in0, s1, op0, rd, wr, s2=None, op1=None, eng="dve"):
        if op1 is None:
            return P.op(eng, lambda e: e.tensor_scalar(out=out, in0=in0, scalar1=s1, scalar2=None, op0=op0), rd=rd, wr=wr)
        return P.op(eng, lambda e: e.tensor_scalar(out=out, in0=in0, scalar1=s1, scalar2=s2, op0=op0, op1=op1), rd=rd, wr=wr)

    def stt(out, in0, scalar, in1, op0, op1, rd, wr):
        return P.op("dve", lambda e: e.scalar_tensor_tensor(out=out, in0=in0, scalar=scalar, in1=in1, op0=op0, op1=op1),
                    rd=rd, wr=wr)

    def dma(q, out, in_, rd, wr):
        return P.op(q, lambda e: e.dma_start(out=out, in_=in_), rd=rd, wr=wr, dma=True)

    wctr = [0]

    def wget(nm, idx, sz):
        k = wctr[0] % NSLOT
        wctr[0] += 1
        dma("pool", wsl[k][:, 0:sz], wb[nm][idx], rd=[wbuf[nm][idx]], wr=[wslb[k]])
        return wsl[k], wslb[k]

    import os
    KSTOP = int(os.environ.get("KSTOP", "99"))

    class _Stop(Exception):
        pass

    def chk(n):
        if KSTOP == n:
            P.barrier()
            raise _Stop()

    def body():
        cur[0] = phase_base
        dstage = [sb(f"dstage{i}", [128, KW * 128], BF16) for i in range(2)]
        dstb = [Buf(), Buf()]
        lq = sb("lq", [64, 4 * L], F32)
        lpr = sb("lpr", [64, 2 * L], F32)
        setup_b = Buf()

        cb_id = Buf()
        cb_kc = Buf()
        dma("sp", vecs[:], vecs_d, rd=[], wr=[cbuf])
        dma("sp", identf[:], ident_d, rd=[], wr=[cb_id])
        dma("sp", kct[:], kc_d, rd=[], wr=[cb_kc])
        dma("sp", lq[:], lam_d, rd=[], wr=[setup_b])
        chk(0)
        for (nm, nb, sz) in wnames:
            for i in range(L * nb):
                dma("pool", wb[nm][i], wf[nm][i], rd=[], wr=[wbuf[nm][i]])
        chk(1)
        cb2 = Buf()
        P.op("dve", lambda e: e.tensor_copy(out=identb[:], in_=identf[:]), rd=[cb_id], wr=[cb2])
        P.op("dve", lambda e: e.memset(onesb[:], 1.0), rd=[], wr=[cb2])
        P.op("dve", lambda e: e.memset(onesf[:], 1.0), rd=[], wr=[cb2])
        P.op("dve", lambda e: e.memset(epst[:, 0:1], LN_EPS / (ALPHA * ALPHA)), rd=[], wr=[cb2])
        P.op("dve", lambda e: e.memset(epst[:, 1:2], LN_EPS), rd=[], wr=[cb2])
        for l in range(L):
            for c in range(8):
                k = (l * 8 + c) % 2

                def fn(e, l=l, c=c, k=k):
                    ins = None
                    for j in range(KW):
                        ins = e.tensor_scalar(out=dstage[k][:, j * 128:(j + 1) * 128], in0=identb[:],
                                              scalar1=vcol(l, V_CDW, j * 8 + c), scalar2=None, op0=ALU.mult)
                    return ins
                P.op("dve", fn, rd=[cbuf, cb2], wr=[dstb[k]])
                dma("sp", wb["cv"][l * 8 + c], dstage[k][:], rd=[dstb[k]], wr=[wbuf["cv"][l * 8 + c]])
        chk(2)
        P.op("dve", lambda e: e.tensor_tensor(out=lpr[:, 0:L], in0=lq[:, 0:L], in1=lq[:, L:2 * L], op=ALU.mult),
             rd=[setup_b], wr=[setup_b])
        P.op("dve", lambda e: e.tensor_tensor(out=lpr[:, L:2 * L], in0=lq[:, 2 * L:3 * L], in1=lq[:, 3 * L:4 * L], op=ALU.mult),
             rd=[setup_b], wr=[setup_b])
        mm_group(psb[0][:, 0:2 * L], [(onesf[0:64, :], lpr[:, :])], rd=[setup_b, cb2], wr=[psbuf[0]])
        act(lamt[:, 2 * L:4 * L], psb[0][:, 0:2 * L], AF.Exp, rd=[psbuf[0]], wr=[setup_b])
        tt(lamt[:, 0:L], lamt[:, 3 * L:4 * L], lamt[:, 2 * L:3 * L], ALU.subtract, rd=[setup_b], wr=[setup_b])
        tt(lamt[:, 0:L], lamt[:, 0:L], kct[:, 0:L], ALU.subtract, rd=[setup_b, cb_kc], wr=[setup_b])
        for l in range(L):
            tt(lamt[:, L + l:L + l + 1], vcol(l, V_SUBG), kct[:, L + l:L + l + 1], ALU.mult, rd=[cbuf, cb_kc, setup_b], wr=[setup_b])
        chk(3)
        P.barrier()

        cur[0] = phase_base
        xf = sb("xf", [128, 8, TT], F32)
        xb = sb("xb", [128, 8, TT], BF16)
        tf = sb("tf", [128, 8, TT], F32)
        hb = sb("hb", [128, NFC, TT], BF16)
        plef = sb("plef", [128, 8, TT], F32)
        gts = sb("gts", [128, 16, TT], BF16)
        att = sb("att", [128, 8, TT], BF16)
        chalo = sb("chalo", [128, 8, TT + 32], BF16)
        stg = sb("stg", [128, 4, TT], BF16)
        stat = sb("stat", [128, 5, TT], F32)
        tmpf = sb("tmpf", [128, 4, TT], F32)
        cosb = sb("cosb", [128, 2, TT], F32)
        sinb = sb("sinb", [128, 2, TT], F32)
        oring = sb("oring", [128, 6, TT], BF16)
        pstage = sb("pstage", [128, 4, PLE], F32)
        ptb = sb("ptb", [128, 2, TT], BF16)
        ac_end = cur[0]

        xfb = [Buf() for _ in range(8)]
        xbb = [Buf() for _ in range(8)]
        tfb = [Buf() for _ in range(8)]
        hbb = [Buf() for _ in range(NFC)]
        plb = [Buf() for _ in range(8)]
        gtb = [Buf() for _ in range(16)]
        atb = [Buf() for _ in range(8)]
        chb = [Buf() for _ in range(8)]
        stgb = [Buf() for _ in range(4)]
        statb = [Buf() for _ in range(5)]
        tmpb = [Buf() for _ in range(4)]
        csb = [Buf(), Buf()]
        snb = [Buf(), Buf()]
        orb = [Buf() for _ in range(6)]
        pstb = Buf()
        ptbb = [Buf(), Buf()]

        rot = {"bank": 0, "tmp": 0, "or": 0, "stg": 0}

        def bank():
            k = rot["bank"]
            rot["bank"] = (k + 1) % 6
            return psb[k], psbuf[k]

        def tmp():
            k = rot["tmp"]
            rot["tmp"] = (k + 1) % 4
            return tmpf[:, k, :], tmpb[k]

        def oring_next():
            k = rot["or"]
            rot["or"] = (k + 1) % 6
            return oring[:, k, :], orb[k]

        S1p, S1b, S2p, S2b = psb[6], psbuf[6], psb[7], psbuf[7]

        pend = []

        def stats_pe(k, first, last):
            mm_group(S1p[:], [(onesb[:], stg[:, k, :])], rd=[stgb[k]], wr=[S1b], start=first, stop=last)
            mm_group(S2p[:], [(onesb[:], stg[:, 2 + k, :])], rd=[stgb[2 + k]], wr=[S2b], start=first, stop=last)

        def stats_flush():
            while pend:
                stats_pe(*pend.pop(0))

        def ln_stats_chunk(dc, first, last):
            k = rot["stg"]
            rot["stg"] = (k + 1) % 2
            act(stg[:, k, :], tf[:, dc, :], AF.Copy, rd=[tfb[dc]], wr=[stgb[k]])
            act(stg[:, 2 + k, :], tf[:, dc, :], AF.Square, rd=[tfb[dc]], wr=[stgb[2 + k]])
            pend.append((k, first, last))

        def stats_prev():
            while len(pend) > 1:
                stats_pe(*pend.pop(0))

        def ln_finish(l, goff, boff, epscol, mode):
            stats_flush()
            mean, msq, var, lnv, rstd = [stat[:, i, :] for i in range(5)]
            ts(mean, S1p[:], 1.0 / D, ALU.mult, rd=[S1b], wr=[statb[0]])
            tt(msq, mean, mean, ALU.mult, rd=[statb[0]], wr=[statb[1]])
            stt(var, S2p[:], 1.0 / D, msq, ALU.mult, ALU.subtract, rd=[S2b, statb[1]], wr=[statb[2]])
            act(lnv, var, AF.Ln, rd=[statb[2]], wr=[statb[3]], bias=epst[:, epscol:epscol + 1])
            act(rstd, lnv, AF.Exp, rd=[statb[3]], wr=[statb[4]], scale=-0.5)
            for dc in range(8):
                eng = "dve"
                tt(tf[:, dc, :], tf[:, dc, :], mean, ALU.subtract, rd=[tfb[dc], statb[0]], wr=[tfb[dc]], eng=eng)
                tt(tf[:, dc, :], tf[:, dc, :], rstd, ALU.mult, rd=[tfb[dc], statb[4]], wr=[tfb[dc]], eng=eng)
                g = vcol(l, goff, dc)
                b = vcol(l, boff, dc)
                if mode == "x":
                    act(xb[:, dc, :], tf[:, dc, :], AF.Identity, rd=[tfb[dc]], wr=[xbb[dc]], bias=b, scale=g)
                else:
                    act(hb[:, dc, :], tf[:, dc, :], AF.Silu, rd=[tfb[dc]], wr=[hbb[dc]], bias=b, scale=g)
            if mode == "x":
                for dc in range(8):
                    act(xf[:, dc, :], tf[:, dc, :], AF.Identity, rd=[tfb[dc]], wr=[xfb[dc]],
                        bias=vcol(l, boff, dc), scale=vcol(l, goff, dc))

        def ffn(l, which, ple):
            gu = "gu1" if which == 1 else "gu2"
            dd = "d1" if which == 1 else "d2"
            for j in range(11):
                W, Wb = wget(gu, l * 11 + j, 4096)
                Wv = W[:, 0:4096].rearrange("p (k n) -> p k n", k=8)
                for s in range(2):
                    fc = 2 * j + s
                    pg, pgb = bank()
                    pu, pub = bank()
                    mm_group(pg[:], [(Wv[:, dc, s * 128:(s + 1) * 128], xb[:, dc, :]) for dc in range(8)], rd=[Wb] + xbb, wr=[pgb])
                    mm_group(pu[:], [(Wv[:, dc, 256 + s * 128:256 + (s + 1) * 128], xb[:, dc, :]) for dc in range(8)],
                             rd=[Wb] + xbb, wr=[pub])
                    t, tb_ = tmp()
                    act(t, pg[:], AF.Silu, rd=[pgb], wr=[tb_])
                    tt(hb[:, fc, :], t, pu[:], ALU.mult, rd=[tb_, pub], wr=[hbb[fc]])
            for j in range(4):
                W, Wb = wget(dd, l * 4 + j, 5632)
                Wv = W[:, 0:5632].rearrange("p (k n) -> p k n", k=NFC)
                for s in range(2):
                    dc = 2 * j + s
                    po, pob = bank()
                    mm_group(po[:], [(Wv[:, fc, s * 128:(s + 1) * 128], hb[:, fc, :]) for fc in range(NFC)], rd=[Wb] + hbb, wr=[pob])
                    stats_prev()
                    stt(tf[:, dc, :], po[:], 0.5 / ALPHA, xf[:, dc, :], ALU.mult, ALU.add, rd=[pob, xfb[dc]], wr=[tfb[dc]])
                    if ple:
                        stt(tf[:, dc, :], plef[:, dc, :], 1.0 / ALPHA, tf[:, dc, :], ALU.mult, ALU.add,
                            rd=[plb[dc], tfb[dc]], wr=[tfb[dc]])
                    ln_stats_chunk(dc, dc == 0, dc == 7)

        def load_x_tile(i):
            tok0 = tiles[i][0]
            xsv = plef[:].rearrange("p c t -> p (c t)").rearrange("p (s f) -> p s f", s=4)
            dma("sp", xsv, x_d[tok0:tok0 + TT, :].rearrange("(s p) f -> p s f", p=128), rd=[], wr=plb)

        def input_transpose(i):
            tfv = plef[:].rearrange("p c t -> p (c t)").rearrange("p (s f) -> p s f", s=4)
            for dc in range(8):
                pb_, pbb = bank()

                def fn(e, dc=dc, pb_=pb_):
                    ins = None
                    for s in range(4):
                        ins = e.transpose(pb_[:, s * 128:(s + 1) * 128], tfv[:, s, dc * 128:(dc + 1) * 128], identf[:])
                    return ins
                P.op("pe", fn, rd=plb, wr=[pbb])
                chk(32)
                act(xf[:, dc, :], pb_[:], AF.Copy, rd=[pbb], wr=[xfb[dc]])
                chk(33)
                P.op("dve", lambda e, dc=dc: e.tensor_copy(out=xb[:, dc, :], in_=xf[:, dc, :]), rd=[xfb[dc]], wr=[xbb[dc]])
                chk(34 + dc)

        def output_store(i):
            tok0 = tiles[i][0]
            tfv = tf[:].rearrange("p c t -> p (c t)").rearrange("p (s f) -> p s f", s=4)
            for s in range(4):
                for half in range(2):
                    pb_, pbb = bank()

                    def fn(e, s=s, half=half, pb_=pb_):
                        ins = None
                        for q4 in range(4):
                            dc = half * 4 + q4
                            ins = e.transpose(pb_[:, q4 * 128:(q4 + 1) * 128], xf[:, dc, s * 128:(s + 1) * 128], identf[:])
                        return ins
                    P.op("pe", fn, rd=xfb, wr=[pbb])
                    if half == 0:
                        act(tfv[:, s, 0:512], pb_[:], AF.Copy, rd=[pbb], wr=tfb)
                    else:
                        P.op("dve", lambda e, s=s, pb_=pb_: e.tensor_copy(out=tfv[:, s, 512:1024], in_=pb_[:]), rd=[pbb], wr=tfb)
            dma("sp", y_d[tok0:tok0 + TT, :].rearrange("(s p) f -> p s f", p=128), tfv, rd=tfb, wr=[])

        def win_stage(l, i):
            tok0, pos0, st, ln = tiles[i]
            par = l % 2
            dma("sp", x1s[:, :, tok0:tok0 + TT].rearrange("c p t -> p c t"), xf[:], rd=xfb, wr=[])
            ck = i % 2
            dma("sp", cosb[:, ck, :], cos_d[:, pos0:pos0 + TT], rd=[], wr=[csb[ck]])
            dma("sp", sinb[:, ck, :], sin_d[:, pos0:pos0 + TT], rd=[], wr=[snb[ck]])
            for qk in range(2):
                dst = qs if qk == 0 else ks
                for hp in range(4):
                    W, Wb = wget("win", l * 18 + qk * 4 + hp, 4096)
                    Wv = W[:, 0:4096].rearrange("p (k n) -> p k n", k=8)
                    for s in range(2):
                        h = 2 * hp + s
                        pz, pzb = bank()
                        pw, pwb = bank()
                        mm_group(pz[:], [(Wv[:, dc, s * 256:s * 256 + 128], xb[:, dc, :]) for dc in range(8)], rd=[Wb] + xbb, wr=[pzb])
                        mm_group(pw[:], [(Wv[:, dc, s * 256 + 128:s * 256 + 256], xb[:, dc, :]) for dc in range(8)],
                                 rd=[Wb] + xbb, wr=[pwb])
                        a, ab = tmp()
                        tt(a, pz[:], cosb[:, ck, :], ALU.mult, rd=[pzb, csb[ck]], wr=[ab])
                        b2, bb = tmp()
                        tt(b2, pw[:], sinb[:, ck, :], ALU.mult, rd=[pwb, snb[ck]], wr=[bb])
                        o, ob = oring_next()
                        tt(o, a, b2, ALU.add, rd=[ab, bb], wr=[ob])
                        dma("sp", dst[h][:, tok0:tok0 + TT], o, rd=[ob], wr=[])
            for cp in range(4):
                W, Wb = wget("win", l * 18 + 8 + cp, 4096)
                Wv = W[:, 0:4096].rearrange("p (k n) -> p k n", k=8)
                for s in range(2):
                    c = 2 * cp + s
                    pa, pab = bank()
                    pb_, pbb = bank()
                    mm_group(pa[:], [(Wv[:, dc, s * 256:s * 256 + 128], xb[:, dc, :]) for dc in range(8)], rd=[Wb] + xbb, wr=[pab])
                    mm_group(pb_[:], [(Wv[:, dc, s * 256 + 128:s * 256 + 256], xb[:, dc, :]) for dc in range(8)],
                             rd=[Wb] + xbb, wr=[pbb])
                    sg, sgb = tmp()
                    act(sg, pb_[:], AF.Sigmoid, rd=[pbb], wr=[sgb])
                    o, ob = oring_next()
                    tt(o, pa[:], sg, ALU.mult, rd=[pab, sgb], wr=[ob])
                    dma("sp", cs[par][c][:, tok0:tok0 + TT], o, rd=[ob], wr=[])
            for gb in range(4):
                W, Wb = wget("win", l * 18 + 12 + gb, 4096)
                Wv = W[:, 0:4096].rearrange("p (k n) -> p k n", k=8)
                for s in range(4):
                    ch = gb * 4 + s
                    pz, pzb = bank()
                    mm_group(pz[:], [(Wv[:, dc, s * 128:(s + 1) * 128], xb[:, dc, :]) for dc in range(8)], rd=[Wb] + xbb, wr=[pzb])
                    o, ob = oring_next()
                    act(o, pz[:], AF.Sigmoid, rd=[pzb], wr=[ob])
                    dma("sp", gs[ch][:, tok0:tok0 + TT], o, rd=[ob], wr=[])
            for vb in range(2):
                W, Wb = wget("win", l * 18 + 16 + vb, 4096)
                Wv = W[:, 0:4096].rearrange("p (k n) -> p k n", k=8)
                for s in range(4):
                    pv, pvb = bank()
                    mm_group(pv[:], [(xb[:, dc, s * 128:(s + 1) * 128], Wv[:, dc, :]) for dc in range(8)], rd=[Wb] + xbb, wr=[pvb])
                    o, ob = oring_next()
                    if s % 2 == 0:
                        act(o, pv[:], AF.Copy, rd=[pvb], wr=[ob])
                    else:
                        P.op("dve", lambda e, o=o, pv=pv: e.tensor_copy(out=o, in_=pv[:]), rd=[pvb], wr=[ob])
                    dma("sp", vs[tok0 + s * 128:tok0 + (s + 1) * 128, vb * 512:(vb + 1) * 512], o, rd=[ob], wr=[])

        def load_chalo(l, i):
            tok0, pos0, st, ln = tiles[i]
            par = l % 2
            lo = 16 if pos0 == 0 else 0
            hi = TT + 16 if pos0 + TT >= ln else TT + 32
            if lo > 0:
                P.op("dve", lambda e: e.memset(chalo[:, :, 0:16], 0.0), rd=[], wr=chb)
            if hi < TT + 32:
                P.op("dve", lambda e: e.memset(chalo[:, :, TT + 16:TT + 32], 0.0), rd=[], wr=chb)
            g0 = tok0 - 16 + lo
            dma("sp", chalo[:, :, lo:hi], cs[par][:, :, g0:g0 + (hi - lo)].rearrange("c p t -> p c t"), rd=[], wr=chb)

        def load_gts_att(i):
            tok0 = tiles[i][0]
            dma("sp", att[:], ats[:, :, tok0:tok0 + TT].rearrange("c p t -> p c t"), rd=[], wr=atb)
            dma("sp", gts[:], gs[:, :, tok0:tok0 + TT].rearrange("c p t -> p c t"), rd=[], wr=gtb)

        def load_p(l, i):
            tok0 = tiles[i][0]
            dma("sp", pstage[:], p_d[l, tok0:tok0 + TT, :].rearrange("(s p) f -> p s f", p=128), rd=[], wr=[pstb])

        def load_x1(i):
            tok0 = tiles[i][0]
            dma("sp", xf[:], x1s[:, :, tok0:tok0 + TT].rearrange("c p t -> p c t"), rd=[], wr=xfb)

        def phase_c_tile(l, i, nxt):
            for c in range(8):
                W, Wb = wget("cv", l * 8 + c, KW * 128)
                pc, pcb = bank()
                mm_group(pc[:], [(W[:, j * 128:(j + 1) * 128], chalo[:, c, 1 + j:1 + j + TT]) for j in range(KW)],
                         rd=[Wb, chb[c]], wr=[pcb])
                stats_prev()
                act(tf[:, c, :], pc[:], AF.Identity, rd=[pcb], wr=[tfb[c]], bias=vcol(l, V_CDWB, c))
                ln_stats_chunk(c, c == 0, c == 7)
            if nxt is not None:
                load_chalo(l, nxt)
            ln_finish(l, V_CLNG, V_CLNB, 1, "silu")
            for nb in range(2):
                W, Wb = wget("pw2", l * 2 + nb, 4096)
                Wv = W[:, 0:4096].rearrange("p (k n) -> p k n", k=8)
                for s in range(4):
                    dc = nb * 4 + s
                    po, pob = bank()
                    mm_group(po[:], [(Wv[:, cc, s * 128:(s + 1) * 128], hb[:, cc, :]) for cc in range(8)], rd=[Wb] + hbb[0:8], wr=[pob])
                    t1, t1b = tmp()
                    tt(t1, po[:], gts[:, 8 + dc, :], ALU.mult, rd=[pob, gtb[8 + dc]], wr=[t1b])
                    t2, t2b = tmp()
                    tt(t2, att[:, dc, :], gts[:, dc, :], ALU.mult, rd=[atb[dc], gtb[dc]], wr=[t2b])
                    tt(hb[:, 8 + dc, :], t1, t2, ALU.add, rd=[t1b, t2b], wr=[hbb[8 + dc]])
            if nxt is not None:
                load_gts_att(nxt)
            for nb in range(2):
                W, Wb = wget("wo", l * 2 + nb, 4096)
                Wv = W[:, 0:4096].rearrange("p (k n) -> p k n", k=8)
                for s in range(4):
                    dc = nb * 4 + s
                    po, pob = bank()
                    mm_group(po[:], [(Wv[:, mc, s * 128:(s + 1) * 128], hb[:, 8 + mc, :]) for mc in range(8)],
                             rd=[Wb] + hbb[8:16], wr=[pob])
                    stats_prev()
                    stt(tf[:, dc, :], po[:], 1.0 / ALPHA, xf[:, dc, :], ALU.mult, ALU.add, rd=[pob, xfb[dc]], wr=[tfb[dc]])
                    ln_stats_chunk(dc, dc == 0, dc == 7)
            ln_finish(l, V_LN2G, V_LN2B, 0, "x")
            for pc2 in range(2):
                pb_, pbb = bank()

                def fn(e, pc2=pc2, pb_=pb_):
                    ins = None
                    for s in range(4):
                        ins = e.transpose(pb_[:, s * 128:(s + 1) * 128], pstage[:, s, pc2 * 128:(pc2 + 1) * 128], identf[:])
                    return ins
                P.op("pe", fn, rd=[pstb], wr=[pbb])
                act(ptb[:, pc2, :], pb_[:], AF.Copy, rd=[pbb], wr=[ptbb[pc2]])
            if nxt is not None:
                load_p(l, nxt)
            Wp, Wpb = wget("pp", l, 2048)
            Wpv = Wp[:, 0:2048].rearrange("p (k n) -> p k n", k=2)
            for nb in range(2):
                W, Wb = wget("pg", l * 2 + nb, 4096)
                Wv = W[:, 0:4096].rearrange("p (k n) -> p k n", k=8)
                for s in range(4):
                    dc = nb * 4 + s
                    pj, pjb = bank()
                    pgt, pgtb = bank()
                    mm_group(pj[:], [(Wpv[:, k2, dc * 128:(dc + 1) * 128], ptb[:, k2, :]) for k2 in range(2)], rd=[Wpb] + ptbb, wr=[pjb])
                    mm_group(pgt[:], [(Wv[:, cc, s * 128:(s + 1) * 128], xb[:, cc, :]) for cc in range(8)], rd=[Wb] + xbb, wr=[pgtb])
                    sg, sgb = tmp()
                    act(sg, pgt[:], AF.Sigmoid, rd=[pgtb], wr=[sgb])
                    tt(plef[:, dc, :], pj[:], sg, ALU.mult, rd=[pjb, sgb], wr=[plb[dc]])
            ffn(l, 2, True)
            ln_finish(l, V_LN3G, V_LN3B, 0, "x")

        load_x_tile(0)
        chk(31)
        for i in range(NT):
            input_transpose(i)
            chk(4)
            if i + 1 < NT:
                load_x_tile(i + 1)
            ffn(0, 1, False)
            chk(5)
            ln_finish(0, V_LN1G, V_LN1B, 0, "x")
            chk(6)
            win_stage(0, i)
            chk(7)
        P.barrier()
        chk(8)

        cur[0] = phase_base
        kT = [sb(f"kT{i}", [128, SMAX], BF16) for i in range(2)]
        vv = [sb(f"vv{i}", [128, SMAX // 128, 128], BF16) for i in range(2)]
        qt = [sb(f"qt{i}", [128, TT], BF16) for i in range(2)]
        pT = sb("pT", [128, 6, TT], BF16)
        accd = sb("accd", [128, 2, TT], F32)
        accp = sb("accp", [128, 2, TT], F32)
        nrm = sb("nrm", [128, 9, TT], F32)
        osq = sb("osq", [128, TT], BF16)
        ores = sb("ores", [128, 2, TT], BF16)
        kvb = [Buf(), Buf()]
        vvb = [Buf(), Buf()]
        qtb = [Buf(), Buf()]
        pTb = [Buf() for _ in range(6)]
        accb_ = {("dve", 0): Buf(), ("dve", 1): Buf(), ("pool", 0): Buf(), ("pool", 1): Buf()}
        nrb = [Buf() for _ in range(9)]
        osqb = Buf()
        oresb = [Buf(), Buf()]

        def attention(l):
            units = []
            job = 0
            for (st, ln) in seqs:
                for h in range(NH):
                    for qi in range(ln // TT):
                        units.append((job, st, ln, h, qi))
                    job += 1

            def load_kv(u):
                jb, st, ln, h, qi = u
                k = jb % 2
                dma("sp", kT[k][:, 0:ln], ks[h][:, st:st + ln], rd=[], wr=[kvb[k]])
                dma("sp", vv[k][:, 0:ln // 128, :], vs[st:st + ln, h * 128:(h + 1) * 128].rearrange("(c p) e -> p c e", p=128),
                    rd=[], wr=[vvb[k]])

            def load_q(ui):
                jb, st, ln, h, qi = units[ui]
                dma("sp", qt[ui % 2][:], qs[h][:, st + qi * TT:st + (qi + 1) * TT], rd=[], wr=[qtb[ui % 2]])

            load_kv(units[0])
            load_q(0)
            prot = [0]
            for ui, u in enumerate(units):
                jb, st, ln, h, qi = u
                if ui + 1 < len(units):
                    if units[ui + 1][0] != jb:
                        load_kv(units[ui + 1])
                    load_q(ui + 1)
                kb = jb % 2
                q = qt[ui % 2]
                qb_ = qtb[ui % 2]
                nk = ln // 128
                pmap = {}
                used = {}

                def who(j, m):
                    if (j + m) % 2 == 0:
                        return "pe"
                    idx = (j - (1 - m)) // 2
                    return "pool" if idx % 4 == 3 else "dve"

                def QK(j):
                    for m in range(2):
                        bk = 4 + (j % 2) * 2 + m
                        mm_group(psb[bk][:], [(kT[kb][64 * m:64 * m + 64, j * 128:(j + 1) * 128], q[64 * m:64 * m + 64, :])],
                                 rd=[kvb[kb], qb_], wr=[psbuf[bk]])

                def EXP(j):
                    for m in range(2):
                        bk = 4 + (j % 2) * 2 + m
                        r = prot[0]
                        prot[0] = (r + 1) % 6
                        pmap[(j, m)] = r
                        act(pT[:, r, :], psb[bk][:], AF.Exp, rd=[psbuf[bk]], wr=[pTb[r]], scale=0.125)

                def PVL(j):
                    for m in range(2):
                        r = pmap[(j, m)]
                        w = who(j, m)
                        mm_group(psb[m][:], [(vv[kb][:, j, :], pT[:, r, :])], rd=[vvb[kb], pTb[r]], wr=[psbuf[m]],
                                 start=(j == 0), stop=(j == nk - 1))
                        if w == "pe":
                            mm_group(psb[2 + m][:], [(onesb[:], pT[:, r, :])], rd=[pTb[r]], wr=[psbuf[2 + m]],
                                     start=(j == m), stop=False)
                        else:
                            acc = (accp if w == "pool" else accd)[:, m, :]
                            ab = accb_[(w, m)]
                            src = pT[:, r, :]
                            if not used.get((w, m)):
                                used[(w, m)] = True
                                P.op(w, lambda e, acc=acc, src=src: e.tensor_copy(out=acc, in_=src), rd=[pTb[r]], wr=[ab])
                            else:
                                P.op(w, lambda e, acc=acc, src=src: e.tensor_tensor(out=acc, in0=acc, in1=src, op=ALU.add),
                                     rd=[ab, pTb[r]], wr=[ab])

                QK(0)
                for j in range(nk):
                    if j + 1 < nk:
                        QK(j + 1)
                    EXP(j)
                    PVL(j)
                    if j == min(6, nk - 1) and pendn:
                        norm_part2(*pendn.pop())
                for m in range(2):
                    pairs = []
                    rdl = []
                    for w, accx in (("dve", accd), ("pool", accp)):
                        if used.get((w, m)):
                            pairs.append((onesf[:], accx[:, m, :]))
                            rdl.append(accb_[(w, m)])
                    mm_group(psb[2 + m][:], pairs, rd=rdl, wr=[psbuf[2 + m]], start=False, stop=True)
                N = [nrm[:, i, :] for i in range(9)]
                act(N[2], psb[0][:], AF.Copy, rd=[psbuf[0]], wr=[nrb[2]])
                P.op("dve", lambda e: e.tensor_copy(out=N[3], in_=psb[1][:]), rd=[psbuf[1]], wr=[nrb[3]])
                act(N[0], psb[2][:], AF.Copy, rd=[psbuf[2]], wr=[nrb[0]])
                P.op("dve", lambda e: e.tensor_copy(out=N[1], in_=psb[3][:]), rd=[psbuf[3]], wr=[nrb[1]])
                P.op("dve", lambda e: e.reciprocal(out=N[4], in_=N[0]), rd=[nrb[0]], wr=[nrb[4]])
                P.op("dve", lambda e: e.reciprocal(out=N[5], in_=N[1]), rd=[nrb[1]], wr=[nrb[5]])
                tt(N[6], N[2], N[4], ALU.mult, rd=[nrb[2], nrb[4]], wr=[nrb[6]])
                tt(N[7], N[3], N[5], ALU.mult, rd=[nrb[3], nrb[5]], wr=[nrb[7]])
                stt(N[6], N[7], lamt[:, l:l + 1], N[6], ALU.mult, ALU.add, rd=[nrb[7], nrb[6]], wr=[nrb[6]])
                tt(osq[:], N[6], N[6], ALU.mult, rd=[nrb[6]], wr=[osqb])
                pendn.append((l, ui % 2, h, st + qi * TT))
            while pendn:
                norm_part2(*pendn.pop())

        pendn = []

        def norm_part2(l, ok, h, tok0):
            N = [nrm[:, i, :] for i in range(9)]
            mm_group(psb[4][:], [(onesb[:], osq[:])], rd=[osqb], wr=[psbuf[4]])
            act(N[8], psb[4][:], AF.Ln, rd=[psbuf[4]], wr=[nrb[8]], bias=epst[:, 1:2], scale=1.0 / 128)
            act(N[8], N[8], AF.Exp, rd=[nrb[8]], wr=[nrb[8]], scale=-0.5)
            stt(ores[:, ok, :], N[6], lamt[:, L + l:L + l + 1], N[8], ALU.mult, ALU.mult, rd=[nrb[6], nrb[8]], wr=[oresb[ok]])
            dma("sp", ats[h][:, tok0:tok0 + TT], ores[:, ok, :], rd=[oresb[ok]], wr=[])

        for l in range(L):
            attention(l)
            chk(9)
            P.barrier()
            load_chalo(l, 0)
            load_gts_att(0)
            load_p(l, 0)
            load_x1(0)
            for i in range(NT):
                nxt = i + 1 if i + 1 < NT else None
                phase_c_tile(l, i, nxt)
                chk(10)
                if l + 1 < L:
                    ffn(l + 1, 1, False)
                    ln_finish(l + 1, V_LN1G, V_LN1B, 0, "x")
                    win_stage(l + 1, i)
                else:
                    output_store(i)
                if nxt is not None:
                    load_x1(nxt)
            P.barrier()

    try:
        body()
    except _Stop:
        pass

    from contextlib import ExitStack
    with ExitStack() as es:
        csem = {e: es.enter_context(nc.semaphore("c_" + e)) for e in ("pe", "act", "dve", "pool")}
        dsem = {}
        for qn in ("sp", "pool"):
            for k in range(P.ndsem):
                dsem[(qn, k)] = es.enter_context(nc.semaphore(f"d_{qn}{k}"))
        block = es.enter_context(nc.Block())
        P.emit(nc, block, csem, dsem)
    return nc


def _blk(W):
    K, n = W.shape
    kc = K // 128
    return np.ascontiguousarray(W.reshape(kc, 128, n).transpose(1, 0, 2).reshape(128, kc * n))


def _c128(v):
    return np.ascontiguousarray(np.asarray(v).reshape(-1, 128).T)


def _prep_shared(inp, L, SMAX):
    f32 = np.float32
    sh = {}
    g = {k: np.asarray(v, dtype=f32) for k, v in inp.items() if k not in ("x_prompt", "x_sample", "p_prompt", "p_sample")}
    vecs = np.zeros((128, L * VPL), f32)
    for l in range(L):
        o = l * VPL
        for off, nm in ((V_LN1G, "ln1_g"), (V_LN1B, "ln1_b"), (V_CDWB, "conv_dw_b"), (V_CLNG, "conv_ln_g"),
                        (V_CLNB, "conv_ln_b"), (V_LN2G, "ln2_g"), (V_LN2B, "ln2_b"), (V_LN3G, "ln3_g"), (V_LN3B, "ln3_b")):
            vecs[:, o + off:o + off + 8] = _c128(g[nm][l])
        vecs[:, o + V_SUBG:o + V_SUBG + 1] = g["subln_g"][l].reshape(128, 1)
        vecs[:, o + V_CDW:o + V_CDW + KW * 8] = g["conv_dw"][l].reshape(KW * 8, 128).T
    sh["vecs"] = vecs
    lamT = np.zeros((64, 4 * L), f32)
    for wi, nm in enumerate(("lam_q1", "lam_k1", "lam_q2", "lam_k2")):
        lamT[:, wi * L:(wi + 1) * L] = g[nm].T
    sh["lamT"] = lamT
    kc = np.zeros((128, 2 * L), f32)
    for l in range(L):
        li = 0.8 - 0.6 * math.exp(-0.3 * l)
        kc[:, l] = li
        kc[:, L + l] = 1.0 - li
    sh["kc"] = kc
    sh["ident"] = np.eye(128, dtype=f32)
    inv = (f32(10000.0) ** (-np.arange(32, dtype=f32) / f32(32))).astype(f32)
    ang = (np.arange(SMAX, dtype=f32)[None, :] * inv[:, None]).astype(f32)
    cs_, sn_ = np.cos(ang).astype(f32), np.sin(ang).astype(f32)
    sh["cos"] = np.ascontiguousarray(np.tile(cs_, (4, 1)))
    sh["sin"] = np.ascontiguousarray(np.concatenate([-sn_, sn_, -sn_, sn_], axis=0))

    def stack(fn, nb):
        return np.ascontiguousarray(np.stack([fn(l, j) for l in range(L) for j in range(nb)]))

    for which in (1, 2):
        wg, wu, wd = g[f"ffn{which}_w_gate"], g[f"ffn{which}_w_up"], g[f"ffn{which}_w_down"]
        sh[f"w_gu{which}"] = stack(lambda l, j: _blk(np.concatenate([wg[l][:, j * 256:(j + 1) * 256],
                                                                     wu[l][:, j * 256:(j + 1) * 256]], axis=1)), 11)
        sh[f"w_d{which}"] = stack(lambda l, j: _blk(wd[l][:, j * 256:(j + 1) * 256]), 4)
    win = g["w_in"]

    def swp(h0):
        idx = []
        for m in range(2):
            b0 = h0 + m * 64
            idx += list(range(b0 + 32, b0 + 64)) + list(range(b0, b0 + 32))
        return idx

    def win_block(l, j):
        W = win[l]
        if j < 8:
            base = 0 if j < 4 else 1024
            hp = j % 4
            cols = []
            for s in range(2):
                h0 = base + (2 * hp + s) * 128
                cols += list(range(h0, h0 + 128)) + swp(h0)
        elif j < 12:
            cp = j - 8
            cols = []
            for s in range(2):
                c = 2 * cp + s
                cols += list(range(3072 + c * 128, 3072 + (c + 1) * 128)) + list(range(4096 + c * 128, 4096 + (c + 1) * 128))
        elif j < 16:
            gb = j - 12
            cols = list(range(5120 + gb * 512, 5120 + (gb + 1) * 512))
        else:
            vb = j - 16
            cols = list(range(2048 + vb * 512, 2048 + (vb + 1) * 512))
        return _blk(W[:, cols])
    sh["w_win"] = stack(win_block, 18)
    sh["w_pw2"] = stack(lambda l, j: _blk(g["conv_pw2"][l][:, j * 512:(j + 1) * 512]), 2)
    sh["w_wo"] = stack(lambda l, j: _blk(g["w_o"][l][:, j * 512:(j + 1) * 512]), 2)
    sh["w_pg"] = stack(lambda l, j: _blk(g["ple_w_gate"][l][:, j * 512:(j + 1) * 512]), 2)
    sh["w_pp"] = stack(lambda l, j: _blk(g["ple_w_proj"][l]), 1)
    return sh


def kernel(**inputs):
    xs = np.asarray(inputs["x_sample"], dtype=np.float32)
    xp = np.asarray(inputs["x_prompt"], dtype=np.float32)
    ps_ = np.asarray(inputs["p_sample"], dtype=np.float32)
    pp_ = np.asarray(inputs["p_prompt"], dtype=np.float32)
    n = xs.shape[0]
    S0 = xs.shape[1]
    S1 = xp.shape[1]
    L = ps_.shape[0]
    assert xp.shape[0] == 2 * n
    shared = _prep_shared(inputs, L, max(S0, S1))
    nc = build_program(L, S0, S1)
    in_maps = []
    for c in range(n):
        m = dict(shared)
        m["x"] = np.ascontiguousarray(np.concatenate([xs[c], xp[2 * c], xp[2 * c + 1]], axis=0))
        m["p"] = np.ascontiguousarray(np.concatenate([ps_[:, c], pp_[:, 2 * c], pp_[:, 2 * c + 1]], axis=1))
        in_maps.append(m)
    res = run_bass_kernel_spmd(nc, in_maps, core_ids=list(range(n)))
    y_s = np.empty_like(xs)
    y_p = np.empty_like(xp)
    for c in range(n):
        y = res.results[c]["y"]
        y_s[c] = y[0:S0]
        y_p[2 * c] = y[S0:S0 + S1]
        y_p[2 * c + 1] = y[S0 + S1:S0 + 2 * S1]
    return (y_p, y_s)
```
